# Optimizing a Trainium2 kernel written in Bass

```python
import math
import jax, jax.numpy as jnp
from jax import lax
import numpy as np

D_MODEL = 2048
BATCH = 2
SEQ = 8192
DEPTH = 1

D_MIX = D_MODEL
CONV_CH = D_MIX // 2
CONV_GROUPS = 16
CONV_WIDTH = 31
N_HEADS = 16
HEAD_DIM = 64
ATT_WIDTH = N_HEADS * HEAD_DIM
D_IN = 2 * CONV_CH + 3 * ATT_WIDTH
D_FF = 5632
DILATED_PATTERNS = ((128, 1), (512, 4), (2048, 16))
ALIBI_MAX_BIAS = 8.0
EPS = 1e-6

kernel_name = "hymba_conformer_dilated_alibi_layer"


def rms_norm(x, g):
    xf = x.astype(jnp.float32)
    y = xf * lax.rsqrt(jnp.mean(xf * xf, axis=-1, keepdims=True) + EPS)
    return (y * g.astype(jnp.float32)).astype(x.dtype)


def layer_norm(x, g, b):
    xf = x.astype(jnp.float32)
    mu = jnp.mean(xf, axis=-1, keepdims=True)
    var = jnp.mean(jnp.square(xf - mu), axis=-1, keepdims=True)
    y = (xf - mu) * lax.rsqrt(var + EPS)
    return (y * g.astype(jnp.float32) + b.astype(jnp.float32)).astype(x.dtype)


def swiglu_ffn(x, w_gate, w_up, w_down):
    return (jax.nn.silu(x @ w_gate) * (x @ w_up)) @ w_down


def alibi_slopes(n_heads):
    return 2.0 ** (-ALIBI_MAX_BIAS * jnp.arange(1, n_heads + 1, dtype=jnp.float32) / n_heads)


def conformer_conv(u, w_dw, b_dw, ln_g, ln_b):
    a, gate = jnp.split(u, 2, axis=-1)
    v = a * jax.nn.sigmoid(gate)
    v_pad = jnp.pad(v, ((0, 0), (CONV_WIDTH - 1, 0), (0, 0)))
    y = lax.conv_general_dilated(
        v_pad, w_dw[:, None, :].astype(v.dtype), window_strides=(1,), padding="VALID",
        dimension_numbers=("NWC", "WIO", "NWC"), feature_group_count=CONV_CH)
    y = y + b_dw.astype(y.dtype)
    y = layer_norm(y, ln_g, ln_b)
    return jax.nn.silu(y)


def dilated_branch(q, k, v, window, dilation, slopes):
    b, h, s, hd = q.shape
    w = window // dilation
    chunk = dilation * w
    s_pad = -(-s // chunk) * chunk
    n_sub = s_pad // dilation
    nb = n_sub // w

    def to_blocks(t):
        t = jnp.pad(t, ((0, 0), (0, 0), (0, s_pad - s), (0, 0)))
        t = t.reshape(b, h, n_sub, dilation, hd).transpose(0, 1, 3, 2, 4)
        return t.reshape(b, h, dilation, nb, w, hd)

    def with_prev(t):
        prev = jnp.pad(t[:, :, :, :-1], ((0, 0), (0, 0), (0, 0), (1, 0), (0, 0), (0, 0)))
        return jnp.concatenate([prev, t], axis=4)

    qb = to_blocks(q)
    kk = with_prev(to_blocks(k))
    vv = with_prev(to_blocks(v))

    scores = jnp.einsum("bhrnqd,bhrnkd->bhrnqk", qb, kk).astype(jnp.float32)
    scores = scores * (1.0 / math.sqrt(hd))
    qi = jnp.arange(w)[:, None]
    kj = jnp.arange(2 * w)[None, :]
    sub_dist = w + qi - kj
    key_sub_pos = (jnp.arange(nb)[:, None, None] - 1) * w + kj[None]
    valid = (sub_dist >= 0)[None] & (sub_dist <= w)[None] & (key_sub_pos >= 0)
    token_dist = (dilation * sub_dist).astype(jnp.float32)
    bias = -slopes[:, None, None] * token_dist[None]
    scores = scores + bias[None, :, None, None]
    scores = jnp.where(valid[None, None, None], scores, -jnp.inf)

    lse = jax.nn.logsumexp(scores, axis=-1)
    p = jnp.exp(scores - lse[..., None])
    out = jnp.einsum("bhrnqk,bhrnkd->bhrnqd", p, vv.astype(jnp.float32))

    out = out.reshape(b, h, dilation, n_sub, hd).transpose(0, 1, 3, 2, 4).reshape(b, h, s_pad, hd)
    lse = lse.reshape(b, h, dilation, n_sub).transpose(0, 1, 3, 2).reshape(b, h, s_pad)
    return out[:, :, :s], lse[:, :, :s]


def dilated_attention(zq, zk, zv, q_norm_g, k_norm_g):
    b, s, _ = zq.shape
    heads = lambda t: t.reshape(b, s, N_HEADS, HEAD_DIM).transpose(0, 2, 1, 3)
    q = rms_norm(heads(zq), q_norm_g)
    k = rms_norm(heads(zk), k_norm_g)
    v = heads(zv)
    slopes = alibi_slopes(N_HEADS)
    outs, lses = [], []
    for window, dilation in DILATED_PATTERNS:
        o, l = dilated_branch(q, k, v, window, dilation, slopes)
        outs.append(o)
        lses.append(l)
    wts = jax.nn.softmax(jnp.stack(lses, axis=0), axis=0)
    out = jnp.sum(wts[..., None] * jnp.stack(outs, axis=0), axis=0)
    return out.transpose(0, 2, 1, 3).reshape(b, s, ATT_WIDTH).astype(zq.dtype)


def setup_inputs(seed: int = 0) -> dict:
    key = jax.random.key(seed)
    ks = jax.random.split(key, 20)
    f32 = jnp.float32
    nrm = lambda k, shape, fan_in: jax.random.normal(k, shape, f32) * (fan_in ** -0.5)
    gain = lambda k, shape: 1.0 + 0.02 * jax.random.normal(k, shape, f32)
    small = lambda k, shape: 0.02 * jax.random.normal(k, shape, f32)
    L = DEPTH
    return {
        "x": jax.random.normal(ks[0], (BATCH, SEQ, D_MODEL), f32),
        "ffn1_norm_g": gain(ks[1], (L, D_MODEL)),
        "ffn1_w_gate": nrm(ks[2], (L, D_MODEL, D_FF), D_MODEL),
        "ffn1_w_up": nrm(ks[3], (L, D_MODEL, D_FF), D_MODEL),
        "ffn1_w_down": nrm(ks[4], (L, D_FF, D_MODEL), D_FF),
        "mix_norm_g": gain(ks[5], (L, D_MODEL)),
        "w_in": nrm(ks[6], (L, D_MODEL, D_IN), D_MODEL),
        "conv_w_dw": nrm(ks[7], (L, CONV_WIDTH, CONV_CH), CONV_WIDTH),
        "conv_b_dw": small(ks[8], (L, CONV_CH)),
        "conv_ln_g": gain(ks[9], (L, CONV_CH)),
        "conv_ln_b": small(ks[10], (L, CONV_CH)),
        "q_norm_g": gain(ks[11], (L, HEAD_DIM)),
        "k_norm_g": gain(ks[12], (L, HEAD_DIM)),
        "w_out": nrm(ks[13], (L, D_MIX, D_MODEL), D_MIX),
        "ffn2_norm_g": gain(ks[14], (L, D_MODEL)),
        "ffn2_w_gate": nrm(ks[15], (L, D_MODEL, D_FF), D_MODEL),
        "ffn2_w_up": nrm(ks[16], (L, D_MODEL, D_FF), D_MODEL),
        "ffn2_w_down": nrm(ks[17], (L, D_FF, D_MODEL), D_FF),
    }


def reference(x, ffn1_norm_g, ffn1_w_gate, ffn1_w_up, ffn1_w_down, mix_norm_g, w_in,
              conv_w_dw, conv_b_dw, conv_ln_g, conv_ln_b, q_norm_g, k_norm_g, w_out,
              ffn2_norm_g, ffn2_w_gate, ffn2_w_up, ffn2_w_down):
    for l in range(DEPTH):
        x = x + 0.5 * swiglu_ffn(rms_norm(x, ffn1_norm_g[l]), ffn1_w_gate[l], ffn1_w_up[l], ffn1_w_down[l])
        h = rms_norm(x, mix_norm_g[l])
        z = h @ w_in[l]
        c0 = 2 * CONV_CH
        z_conv = z[..., :c0]
        z_q = z[..., c0:c0 + ATT_WIDTH]
        z_k = z[..., c0 + ATT_WIDTH:c0 + 2 * ATT_WIDTH]
        z_v = z[..., c0 + 2 * ATT_WIDTH:]
        y_conv = conformer_conv(z_conv, conv_w_dw[l], conv_b_dw[l], conv_ln_g[l], conv_ln_b[l])
        y_att = dilated_attention(z_q, z_k, z_v, q_norm_g[l], k_norm_g[l])
        y = jnp.concatenate([y_conv, y_att], axis=-1) @ w_out[l]
        x = x + y
        x = x + 0.5 * swiglu_ffn(rms_norm(x, ffn2_norm_g[l]), ffn2_w_gate[l], ffn2_w_up[l], ffn2_w_down[l])
    return x
```

```python
import os
import numpy as np
from contextlib import ExitStack
import concourse.bass as bass
import concourse.mybir as mybir
from concourse.bass_utils import run_bass_kernel_spmd

F32 = mybir.dt.float32
BF16 = mybir.dt.bfloat16
AF = mybir.ActivationFunctionType
ALU = mybir.AluOpType

D = 2048
FF = 5632
KC = 16
FCN = 44
G = 11
NG = 4
TT = 1024
OWN = 2048
HALO = 2048
LOC = OWN + HALO
NCORES = 8
EPS = 1e-6
NEG = -30000.0
DILS = (1, 4, 16)
V_G1, V_GM, V_G2, V_CB, V_LG, V_LB, V_QG, V_KG, V_CW = 0, 16, 32, 48, 56, 64, 72, 73, 74
NV = 74 + 8 * 31


class Sem:
    def __init__(self, nc, name):
        self.h = nc.alloc_semaphore(name)
        self.v = 0


class Buf:
    __slots__ = ("name", "w", "r", "sem", "excl")

    def __init__(self, name, sem=None, excl=False):
        self.name = name
        self.w = None
        self.r = {}
        self.sem = sem
        self.excl = excl


class Trk:
    def __init__(self, nc):
        self.nc = nc
        self.eng = {"pe": nc.tensor, "act": nc.scalar, "dve": nc.vector, "pool": nc.gpsimd, "sp": nc.sync}
        self.esem = {k: Sem(nc, "s_" + k) for k in ["pe", "act", "dve", "pool"]}
        self.waited = {}
        self.dsems = []
        self.free = {False: [], True: []}
        self.local = {False: [], True: []}

    def dbuf(self, name, persistent=False, sw=False):
        free = self.free[sw]
        if persistent or not free:
            s = Sem(self.nc, "d%d_%s" % (len(self.dsems), name))
            self.dsems.append(s)
        else:
            s = free.pop()
        if not persistent:
            self.local[sw].append(s)
        return Buf(name, s)

    def release(self):
        for k in (False, True):
            self.free[k].extend(self.local[k])
            self.local[k] = []

    def store(self, q, fn, src, dst):
        return self.op(q, fn, reads=[src], writes=[dst], sem=src.sem, nowaw=True)

    def _wait(self, e, ev):
        sem, val = ev
        key = (e, id(sem))
        if self.waited.get(key, 0) >= val:
            return
        self.waited[key] = val
        self.eng[e].wait_ge(sem.h, val)

    def op(self, e, fn, reads=(), writes=(), sem=None, nowaw=False):
        if sem is None:
            own, inc = self.esem[e], 1
        else:
            own, inc = sem, 16
        for b in reads:
            if b.w is not None and not (b.w[0] is own and e == "pe"):
                self._wait(e, b.w)
            if b.excl:
                for ev in b.r.values():
                    if ev[0] is not own:
                        self._wait(e, ev)
        for b in writes:
            if b.w is not None and not nowaw and not (b.w[0] is own and e == "pe"):
                self._wait(e, b.w)
            for ev in b.r.values():
                if ev[0] is own and e == "pe":
                    continue
                self._wait(e, ev)
        ins = fn()
        ins.then_inc(own.h, inc)
        own.v += inc
        ev = (own, own.v)
        for b in reads:
            b.r[id(own)] = ev
        for b in writes:
            b.w = ev
            b.r = {}
        return ev

    def dma(self, q, fn, src, dst, nowaw=False):
        return self.op(q, fn, reads=[src] if src is not None else [], writes=[dst], sem=dst.sem, nowaw=nowaw)

    def barrier(self):
        sems = list(self.esem.values()) + self.dsems
        for e in self.eng:
            for s in sems:
                if s.v > 0:
                    self._wait(e, (s, s.v))


class Ring:
    def __init__(self, items):
        self.items = items
        self.i = 0

    def next(self):
        it = self.items[self.i % len(self.items)]
        self.i += 1
        return it


def build_nc(debug=False, stop_after="C"):
    nc = bass.Bass("TRN2", target_bir_lowering=False)
    dt_in = lambda n, s, d=F32: nc.dram_tensor(n, s, d, kind="ExternalInput").ap()
    xin = dt_in("xin", [LOC, D])
    Wg1, Wu1, Wd1 = dt_in("wg1", [D, FF]), dt_in("wu1", [D, FF]), dt_in("wd1", [FF, D])
    Wg2, Wu2, Wd2 = dt_in("wg2", [D, FF]), dt_in("wu2", [D, FF]), dt_in("wd2", [FF, D])
    Win, Wout = dt_in("win", [D, 5120]), dt_in("wout", [D, D])
    vecs_d = dt_in("vecs", [128, NV])
    ident_d = dt_in("ident", [128, 128])
    tab_d = dt_in("tab", [128, 16 * 9 * 128])
    out_d = nc.dram_tensor("out", [OWN, D], F32, kind="ExternalOutput").ap()
    skind = "ExternalOutput" if debug else "Internal"
    x1_d = nc.dram_tensor("x1_d", [2, 128, KC, TT], F32, kind=skind).ap()
    kT_d = nc.dram_tensor("kT_d", [8, 128, LOC], BF16, kind=skind).ap()
    qT_d = nc.dram_tensor("qT_d", [8, 128, OWN], BF16, kind=skind).ap()
    v_d = nc.dram_tensor("v_d", [LOC, 1040], BF16, kind=skind).ap()
    den_d = nc.dram_tensor("den_d", [16, OWN], F32).ap()
    den2_d = nc.dram_tensor("den2_d", [16, OWN], F32).ap()
    u_d = nc.dram_tensor("u_d", [8, 128, 3 * TT], BF16, kind=skind).ap()
    mix_d = nc.dram_tensor("mix_d", [2, 128, KC, TT], BF16, kind=skind).ap()

    with ExitStack() as es:
        t = Trk(nc)
        cnt = [0]

        def sbt(st, n, s, d):
            cnt[0] += 1
            return st.enter_context(nc.sbuf_tensor("sb%d_%s" % (cnt[0], n), s, d))
        P = [es.enter_context(nc.psum_tensor("ps%d" % i, [128, 512], F32)) for i in range(8)]
        PB = [Buf("ps%d" % i, excl=True) for i in range(8)]
        vecs = sbt(es, "vecs", [128, NV], F32); vecs_b = t.dbuf("vecs", True)
        identf = sbt(es, "identf", [128, 128], F32); identf_b = t.dbuf("identf", True)
        identb = sbt(es, "identb", [128, 128], BF16); identb_b = Buf("identb")
        onesb = sbt(es, "onesb", [128, 128], BF16); onesb_b = Buf("onesb")
        blkb = sbt(es, "blkb", [128, 128], BF16); blkb_b = Buf("blkb")
        epsc = sbt(es, "epsc", [128, 1], F32); epsc_b = Buf("epsc")
        c128 = sbt(es, "c128", [128, 1], F32); c128_b = Buf("c128")
        gq8 = sbt(es, "gq8", [128, 1], F32); gq8_b = Buf("gq8")
        cwb = sbt(es, "cwb", [128, 8 * 31], BF16); cwb_b = Buf("cwb")
        out_b = Buf("out")
        x1_b, mix_b = t.dbuf("x1d", True), t.dbuf("mixd", True)
        kT_b, qT_b, v_b, u_b = Buf("kTd"), Buf("qTd"), Buf("vd"), Buf("ud")

        t.dma("sp", lambda: nc.sync.dma_start(out=vecs[:], in_=vecs_d), None, vecs_b)
        t.dma("sp", lambda: nc.sync.dma_start(out=identf[:], in_=ident_d), None, identf_b)
        t.op("dve", lambda: nc.vector.tensor_copy(out=identb[:], in_=identf[:]), reads=[identf_b], writes=[identb_b])
        t.op("dve", lambda: nc.vector.memset(onesb[:], 1.0), writes=[onesb_b])
        t.op("dve", lambda: nc.vector.memset(blkb[:], 0.0), writes=[blkb_b])
        t.op("dve", lambda: nc.vector.memset(blkb[0:64, 0:64], 1.0), writes=[blkb_b])
        t.op("dve", lambda: nc.vector.memset(blkb[64:128, 64:128], 1.0), writes=[blkb_b])
        t.op("dve", lambda: nc.vector.memset(epsc[:], EPS), writes=[epsc_b])
        t.op("dve", lambda: nc.vector.memset(c128[:], 1.0 / 128), writes=[c128_b])
        t.op("dve", lambda: nc.vector.tensor_scalar(out=gq8[:], in0=vecs[:, V_QG:V_QG + 1], scalar1=0.125, scalar2=None,
                                                    op0=ALU.mult), reads=[vecs_b], writes=[gq8_b])
        t.op("dve", lambda: nc.vector.tensor_copy(out=cwb[:], in_=vecs[:, V_CW:V_CW + 248]), reads=[vecs_b], writes=[cwb_b])

        def alloc_ffn_bufs(st):
            B = {}
            B["xT"] = sbt(st, "xT", [128, KC, TT], F32)
            B["xT_b"] = [[Buf("xT%d_%d" % (k, n)) for n in range(2)] for k in range(KC)]
            B["hT"] = sbt(st, "hT", [128, KC, TT], BF16)
            B["hT_b"] = [[Buf("hT%d_%d" % (k, n)) for n in range(2)] for k in range(KC)]
            B["act"] = sbt(st, "act", [128, G, TT], BF16)
            B["act_b"] = [[Buf("act%d_%d" % (k, n)) for n in range(2)] for k in range(G)]
            wgu = [sbt(st, "wgu%d" % i, [128, 2, KC, 128], BF16) for i in range(3)]
            B["wgu"] = Ring([(wgu[i], t.dbuf("wg%d" % i, sw=True), t.dbuf("wu%d" % i, sw=True)) for i in range(3)])
            wd = [sbt(st, "wd%d" % i, [128, G, 128], BF16) for i in range(3)]
            B["wd"] = Ring([(wd[i], t.dbuf("wd%d" % i, sw=True)) for i in range(3)])
            xs = [sbt(st, "xs%d" % i, [128, D], F32) for i in range(2)]
            B["xs"] = Ring([(xs[i], t.dbuf("xs%d" % i)) for i in range(2)])
            sq = [sbt(st, "sq%d" % i, [128, 512], BF16) for i in range(2)]
            B["sq"] = Ring([(sq[i], Buf("sq%d" % i)) for i in range(2)])
            B["rstd"] = sbt(st, "rstd", [128, TT], F32); B["rstd_b"] = [Buf("rstd0"), Buf("rstd1")]
            B["ve"] = sbt(st, "ve", [128, TT], F32); B["ve_b"] = [Buf("ve0"), Buf("ve1")]
            g1 = [sbt(st, "g1_%d" % i, [128, 512], F32) for i in range(2)]
            B["g1"] = Ring([(g1[i], Buf("g1_%d" % i)) for i in range(2)])
            sg = [sbt(st, "sg%d" % i, [128, 512], F32) for i in range(2)]
            B["sg"] = Ring([(sg[i], Buf("sg%d" % i)) for i in range(2)])
            t1 = [sbt(st, "t1_%d" % i, [128, 512], F32) for i in range(2)]
            B["t1"] = Ring([(t1[i], Buf("t1_%d" % i)) for i in range(2)])
            B["gu_banks"] = Ring([(0, 1), (2, 3)])
            B["dn_banks"] = Ring([4, 5])
            B["misc_banks"] = Ring([6, 7])
            B["defer"] = []
            return B

        def flush_defer(B):
            fs, B["defer"] = B["defer"], []
            for f in fs:
                f()

        def evac(idx, out, in_, reads, writes):
            if idx % 2 == 0:
                t.op("act", lambda: nc.scalar.copy(out=out, in_=in_), reads=reads, writes=writes)
            else:
                t.op("dve", lambda: nc.vector.tensor_copy(out=out, in_=in_), reads=reads, writes=writes)

        def make_h(B, gcol, kc, nt, idx):
            xT, xT_b, hT, hT_b = B["xT"], B["xT_b"], B["hT"], B["hT_b"]
            sl = slice(nt * 512, (nt + 1) * 512)
            gap = vecs[:, gcol + kc:gcol + kc + 1]
            if idx % 2 == 0 and not os.environ.get('DBG_NOACTH'):
                t.op("act", lambda: nc.scalar.activation(out=hT[:, kc, sl], in_=xT[:, kc, sl], func=AF.Identity, scale=gap),
                     reads=[xT_b[kc][nt], vecs_b], writes=[hT_b[kc][nt]])
            else:
                t.op("dve", lambda: nc.vector.tensor_scalar(out=hT[:, kc, sl], in0=xT[:, kc, sl], scalar1=gap, scalar2=None,
                                                            op0=ALU.mult), reads=[xT_b[kc][nt], vecs_b], writes=[hT_b[kc][nt]])

        def stat_sq(B, kc, nt, in_act=False):
            xT, xT_b = B["xT"], B["xT_b"]
            sl = slice(nt * 512, (nt + 1) * 512)
            if in_act:
                sq, sq_b = B["act"][:, kc // 2, (kc % 2) * 512:(kc % 2 + 1) * 512], B["act_b"][kc // 2][kc % 2]
            else:
                sq0, sq_b = B["sq"].next()
                sq = sq0[:]
            t.op("act", lambda: nc.scalar.activation(out=sq, in_=xT[:, kc, sl], func=AF.Square), reads=[xT_b[kc][nt]], writes=[sq_b])

            def pe_part():
                t.op("pe", lambda: nc.tensor.matmul(P[6 + nt][:, :], lhsT=onesb[:], rhs=sq, start=(kc == 0), stop=(kc == KC - 1)),
                     reads=[sq_b, onesb_b], writes=[PB[6 + nt]])
            return pe_part

        def stat_finish(B, nt):
            rstd, ve = B["rstd"], B["ve"]
            sl = slice(nt * 512, (nt + 1) * 512)
            t.op("act", lambda: nc.scalar.activation(out=rstd[:, sl], in_=P[6 + nt][:, :], func=AF.Sqrt, bias=epsc[:, 0:1],
                                                     scale=1.0 / D), reads=[PB[6 + nt], epsc_b], writes=[B["rstd_b"][nt]])
            t.op("dve", lambda: nc.vector.scalar_tensor_tensor(out=ve[:, sl], in0=rstd[:, sl], scalar=EPS, in1=rstd[:, sl],
                                                               op0=ALU.mult, op1=ALU.mult),
                 reads=[B["rstd_b"][nt]], writes=[B["ve_b"][nt]])
            t.op("dve", lambda: nc.vector.reciprocal(out=rstd[:, sl], in_=rstd[:, sl]), reads=[B["rstd_b"][nt]],
                 writes=[B["rstd_b"][nt]])

        def load_transpose(B, row0, gcol):
            xT, xT_b = B["xT"], B["xT_b"]
            tr_banks = Ring([4, 5])
            pend = []
            for s in range(TT // 128):
                xs, xs_b = B["xs"].next()
                t.dma("sp", lambda: nc.sync.dma_start(out=xs[:], in_=xin[row0 + s * 128: row0 + (s + 1) * 128, :]), None, xs_b)
                nt = s // 4
                for k4 in range(4):
                    bk = tr_banks.next()

                    def tr():
                        for j in range(4):
                            kc = k4 * 4 + j
                            ins = nc.tensor.transpose(out=P[bk][:, j * 128:(j + 1) * 128], in_=xs[:, kc * 128:(kc + 1) * 128],
                                                      identity=identf[:])
                        return ins
                    t.op("pe", tr, reads=[xs_b, identf_b], writes=[PB[bk]])
                    evac(k4, xT[:, k4 * 4:(k4 + 1) * 4, s * 128:(s + 1) * 128], P[bk][:, :].rearrange("p (j t) -> p j t", j=4),
                         [PB[bk]], [xT_b[k4 * 4 + j][nt] for j in range(4)])
                    if pend:
                        for f in pend[:2]:
                            f()
                        pend = pend[2:]
                        if not pend and not os.environ.get('DBG_NOSTAT'):
                            stat_finish(B, 0)
                if s % 4 == 3:
                    fs = []
                    for kc in range(KC):
                        if not os.environ.get('DBG_NOH'):
                            make_h(B, gcol, kc, nt, kc + 1)
                        if not os.environ.get('DBG_NOSTAT'):
                            fs.append(stat_sq(B, kc, nt, in_act=True))
                    if nt == 0:
                        pend = fs
                    else:
                        B["defer"].append(lambda fs=fs: [f() for f in fs] and None)
            assert not pend
            if not os.environ.get('DBG_NOSTAT'):
                B["defer"].append(lambda: stat_finish(B, 1))

        def gated_pair(B, Wa_v, ca, Wb_v, cb, func, out_fn, out_bufs_fn, cols=((0, 512), (512, 512))):
            hT, hT_b = B["hT"], B["hT_b"]
            rstd, rstd_b = B["rstd"], B["rstd_b"]
            slab, ba, bb = B["wgu"].next()
            t.dma("pool", lambda: nc.gpsimd.dma_start(out=slab[:, 0], in_=Wa_v[:, :, ca:ca + 128]), None, ba)
            t.dma("pool", lambda: nc.gpsimd.dma_start(out=slab[:, 1], in_=Wb_v[:, :, cb:cb + 128]), None, bb)
            for ci, (c0, n) in enumerate(cols):
                sl = slice(c0, c0 + n)
                nt = c0 // 512
                pa, pb = B["gu_banks"].next()
                for (bank, half, wb) in ((pa, 0, ba), (pb, 1, bb)):
                    def mm():
                        for kc in range(KC):
                            ins = nc.tensor.matmul(P[bank][:, 0:n], lhsT=slab[:, half, kc, :], rhs=hT[:, kc, sl],
                                                   start=(kc == 0), stop=(kc == KC - 1))
                        return ins
                    t.op("pe", mm, reads=[wb] + [hT_b[kc][nt] for kc in range(KC)], writes=[PB[bank]])
                flush_defer(B)
                g1, g1_b = B["g1"].next()
                sg, sg_b = B["sg"].next()
                t1, t1_b = B["t1"].next()
                t.op("dve", lambda: nc.vector.tensor_tensor(out=g1[:, 0:n], in0=P[pb][:, 0:n], in1=rstd[:, sl], op=ALU.mult),
                     reads=[PB[pb], rstd_b[nt]], writes=[g1_b])
                t.op("act", lambda: nc.scalar.activation(out=sg[:, 0:n], in_=g1[:, 0:n], func=func), reads=[g1_b], writes=[sg_b])
                t.op("dve", lambda: nc.vector.tensor_tensor(out=t1[:, 0:n], in0=P[pa][:, 0:n], in1=rstd[:, sl], op=ALU.mult),
                     reads=[PB[pa], rstd_b[nt]], writes=[t1_b])
                t.op("dve", lambda: nc.vector.tensor_tensor(out=out_fn(ci), in0=t1[:, 0:n], in1=sg[:, 0:n], op=ALU.mult),
                     reads=[t1_b, sg_b], writes=out_bufs_fn(ci))

        def ffn(B, Wg, Wu, Wd, next_gcol):
            xT, xT_b, act, act_b = B["xT"], B["xT_b"], B["act"], B["act_b"]
            Wg_v = Wg.rearrange("(kc p) f -> p kc f", p=128)
            Wu_v = Wu.rearrange("(kc p) f -> p kc f", p=128)
            Wd_v = Wd.rearrange("(fc p) d -> p fc d", p=128)
            for gi in range(NG):
                for fl in range(G):
                    fc = gi * G + fl
                    gated_pair(B, Wu_v, fc * 128, Wg_v, fc * 128, AF.Silu,
                               lambda nt: act[:, fl, nt * 512:(nt + 1) * 512], lambda nt: [act_b[fl][nt]])
                last = (gi == NG - 1) and next_gcol is not None
                pend = []
                for dc in range(KC):
                    slab, sb_ = B["wd"].next()
                    t.dma("pool", lambda: nc.gpsimd.dma_start(out=slab[:], in_=Wd_v[:, gi * G:(gi + 1) * G, dc * 128:(dc + 1) * 128]),
                          None, sb_)
                    for nt in range(2):
                        sl = slice(nt * 512, (nt + 1) * 512)
                        bk = B["dn_banks"].next()

                        def mm():
                            for fl in range(G):
                                ins = nc.tensor.matmul(P[bk][:, :], lhsT=slab[:, fl, :], rhs=act[:, fl, sl],
                                                       start=(fl == 0), stop=(fl == G - 1))
                            return ins
                        t.op("pe", mm, reads=[sb_] + [act_b[fl][nt] for fl in range(G)], writes=[PB[bk]])
                        t.op("dve", lambda: nc.vector.scalar_tensor_tensor(out=xT[:, dc, sl], in0=P[bk][:, :], scalar=0.5,
                                                                           in1=xT[:, dc, sl], op0=ALU.mult, op1=ALU.add),
                             reads=[PB[bk], xT_b[dc][nt]], writes=[xT_b[dc][nt]])
                    if last:
                        for f in pend:
                            f()
                        pend = []
                        for nt in range(2):
                            make_h(B, next_gcol, dc, nt, nt)
                            pend.append(stat_sq(B, dc, nt))
                if last:
                    def tail(pend=pend):
                        for f in pend:
                            f()
                        stat_finish(B, 0)
                        stat_finish(B, 1)
                    B["defer"].append(tail)

        Win_v = Win.rearrange("(kc p) f -> p kc f", p=128)
        with ExitStack() as stA:
            B = alloc_ffn_bufs(stA)
            st2 = [sbt(stA, "st2_%d" % i, [128, TT], BF16) for i in range(2)]
            st2r = Ring([(st2[i], t.dbuf("st2_%d" % i)) for i in range(2)])
            vst = sbt(stA, "vst", [128, 4, 1040], BF16); vst_b = t.dbuf("vst")
            t.op("dve", lambda: nc.vector.memset(vst[:], 1.0), writes=[vst_b])
            rcol = sbt(stA, "rcol", [128, 8], F32); rcol_b = Buf("rcol")
            hT, hT_b, xT, xT_b = B["hT"], B["hT_b"], B["xT"], B["xT_b"]

            qk_banks = Ring([(0, 1), (2, 3), (4, 5)])

            def qk_jobs(jobs):
                steps = []
                for (col0, gcol_ap, gbuf, dst_ap, dst_b) in jobs:
                    st = {"stg": None}
                    for nt in range(2):
                        steps.append((col0, gcol_ap, gbuf, dst_ap, dst_b, nt, st))
                pend = []

                def stage1(step):
                    (col0, gcol_ap, gbuf, dst_ap, dst_b, nt, st) = step
                    if nt == 0:
                        slab, ba, _bb = B["wgu"].next()
                        t.dma("pool", lambda: nc.gpsimd.dma_start(out=slab[:, 0], in_=Win_v[:, :, col0:col0 + 128]), None, ba)
                        st["slab"], st["ba"] = slab, ba
                        st["stg"], st["stg_b"] = st2r.next()
                    slab, ba = st["slab"], st["ba"]
                    sl = slice(nt * 512, (nt + 1) * 512)
                    pa, pb = qk_banks.next()

                    def mm():
                        for kc in range(KC):
                            ins = nc.tensor.matmul(P[pa][:, :], lhsT=slab[:, 0, kc, :], rhs=hT[:, kc, sl],
                                                   start=(kc == 0), stop=(kc == KC - 1))
                        return ins
                    t.op("pe", mm, reads=[ba] + [hT_b[kc][nt] for kc in range(KC)], writes=[PB[pa]])
                    flush_defer(B)
                    sq, sq_b = B["sq"].next()
                    t.op("act", lambda: nc.scalar.activation(out=sq[:, 0:512], in_=P[pa][:, :], func=AF.Square),
                         reads=[PB[pa]], writes=[sq_b])
                    return (step, pa, pb, sq, sq_b)

                def stage2(item):
                    (step, pa, pb, sq, sq_b) = item
                    (col0, gcol_ap, gbuf, dst_ap, dst_b, nt, st) = step
                    stg, stg_b = st["stg"], st["stg_b"]
                    sl = slice(nt * 512, (nt + 1) * 512)
                    t.op("pe", lambda: nc.tensor.matmul(P[pb][:, :], lhsT=blkb[:], rhs=sq[:, 0:512], start=True, stop=True),
                         reads=[sq_b, blkb_b], writes=[PB[pb]])
                    sg, sg_b = B["sg"].next()
                    t.op("dve", lambda: nc.vector.scalar_tensor_tensor(out=sg[:], in0=P[pb][:, :], scalar=1.0 / 64, in1=B["ve"][:, sl],
                                                                       op0=ALU.mult, op1=ALU.add),
                         reads=[PB[pb], B["ve_b"][nt]], writes=[sg_b])
                    t.op("act", lambda: nc.scalar.activation(out=sg[:], in_=sg[:], func=AF.Sqrt), reads=[sg_b], writes=[sg_b])
                    t.op("dve", lambda: nc.vector.reciprocal(out=sg[:], in_=sg[:]), reads=[sg_b], writes=[sg_b])
                    t.op("dve", lambda: nc.vector.scalar_tensor_tensor(out=stg[:, sl], in0=P[pa][:, :], scalar=gcol_ap, in1=sg[:],
                                                                       op0=ALU.mult, op1=ALU.mult),
                         reads=[PB[pa], sg_b, gbuf], writes=[stg_b])
                    if nt == 1:
                        t.store("sp", lambda: nc.sync.dma_start(out=dst_ap, in_=stg[:]), stg_b, dst_b)

                for step in steps:
                    pend.append(stage1(step))
                    if len(pend) > 1:
                        stage2(pend.pop(0))
                while pend:
                    stage2(pend.pop(0))

            for ti in range(4):
                own = ti >= 2
                load_transpose(B, ti * TT, V_G1)
                if os.environ.get("DBG_STOP") == "load":
                    flush_defer(B)
                    break
                ffn(B, Wg1, Wu1, Wd1, V_GM)
                if os.environ.get("DBG_STOP") == "ffn":
                    flush_defer(B)
                    break
                jobs = [(3072 + c * 128, vecs[:, V_KG:V_KG + 1], vecs_b, kT_d[c, :, ti * TT:(ti + 1) * TT], kT_b) for c in range(8)]
                if own:
                    jobs += [(2048 + c * 128, gq8[:, 0:1], gq8_b, qT_d[c, :, (ti - 2) * TT:(ti - 1) * TT], qT_b) for c in range(8)]
                qk_jobs(jobs)
                if os.environ.get("DBG_STOP") == "qk":
                    break
                for q4 in range(2):
                    def rc():
                        for j in range(4):
                            s_ = q4 * 4 + j
                            ins = nc.tensor.transpose(out=P[6 + q4][:, j * 128:(j + 1) * 128], in_=B["rstd"][:, s_ * 128:(s_ + 1) * 128],
                                                      identity=identf[:])
                        return ins
                    t.op("pe", rc, reads=[B["rstd_b"][q4], identf_b], writes=[PB[6 + q4]])
                    t.op("dve", lambda: nc.vector.tensor_copy(out=rcol[:, q4 * 4:(q4 + 1) * 4], in_=P[6 + q4][:, 0:512:128]),
                         reads=[PB[6 + q4]], writes=[rcol_b])
                for half in range(2):
                    for qc in range(4):
                        slab4, sb_, _bb = B["wgu"].next()
                        slab = slab4[:].rearrange("p a k c -> p (a k c)").rearrange("p (k c) -> p k c", c=256)
                        t.dma("pool", lambda: nc.gpsimd.dma_start(out=slab, in_=Win_v[:, :, 4096 + qc * 256:4096 + (qc + 1) * 256]),
                              None, sb_)
                        for s2 in range(2):
                            bk = B["dn_banks"].next()

                            def mm():
                                for h2 in range(2):
                                    s = half * 4 + s2 * 2 + h2
                                    for kc in range(KC):
                                        ins = nc.tensor.matmul(P[bk][:, h2 * 256:(h2 + 1) * 256], lhsT=hT[:, kc, s * 128:(s + 1) * 128],
                                                               rhs=slab[:, kc, :], start=(kc == 0), stop=(kc == KC - 1))
                                return ins
                            t.op("pe", mm, reads=[sb_] + [hT_b[kc][half] for kc in range(KC)], writes=[PB[bk]])
                            for h2 in range(2):
                                s = half * 4 + s2 * 2 + h2
                                o_ = vst[:, s2 * 2 + h2, qc * 260:(qc + 1) * 260].rearrange("p (h c) -> p h c", c=65)[:, :, 0:64]
                                i_ = P[bk][:, h2 * 256:(h2 + 1) * 256].rearrange("p (h c) -> p h c", c=64)
                                if s2 == 0:
                                    t.op("act", lambda: nc.scalar.activation(out=o_, in_=i_, func=AF.Identity, scale=rcol[:, s:s + 1]),
                                         reads=[PB[bk], rcol_b], writes=[vst_b])
                                else:
                                    t.op("dve", lambda: nc.vector.tensor_scalar(out=o_, in0=i_, scalar1=rcol[:, s:s + 1], scalar2=None,
                                                                                op0=ALU.mult), reads=[PB[bk], rcol_b], writes=[vst_b])
                    r0 = ti * TT + half * 512
                    t.store("sp", lambda: nc.sync.dma_start(out=v_d[r0:r0 + 512, :].rearrange("(s p) c -> p s c", p=128),
                                                            in_=vst[:]), vst_b, v_b)
                if os.environ.get("DBG_STOP") == "v":
                    break
                if ti >= 1:
                    for c in range(8):
                        stg, stg_b = st2r.next()
                        if ti == 1:
                            gated_pair(B, Win_v, c * 128, Win_v, 1024 + c * 128, AF.Sigmoid,
                                       lambda ci: stg[:, TT - 128:TT], lambda ci: [stg_b], cols=((TT - 128, 128),))
                            t.store("sp", lambda: nc.sync.dma_start(out=u_d[c, :, TT - 128:TT], in_=stg[:, TT - 128:TT]), stg_b, u_b)
                        else:
                            gated_pair(B, Win_v, c * 128, Win_v, 1024 + c * 128, AF.Sigmoid,
                                       lambda nt: stg[:, nt * 512:(nt + 1) * 512], lambda nt: [stg_b])
                            t.store("sp", lambda: nc.sync.dma_start(out=u_d[c, :, (ti - 1) * TT:ti * TT], in_=stg[:]), stg_b, u_b)
                if own:
                    t.op("sp", lambda: nc.sync.dma_start(out=x1_d[ti - 2], in_=xT[:]),
                         reads=[xT_b[k][n] for k in range(KC) for n in range(2)], writes=[x1_b], sem=x1_b.sem, nowaw=True)
            t.barrier()
            t.release()

        if stop_after != "A":
            with ExitStack() as stB:
                with ExitStack() as st1:
                  if not os.environ.get('SKIP_CONV'):
                    mixT = sbt(st1, "mixT", [128, 8, OWN], BF16)
                    mixT_b = [[Buf("mix%d_%d" % (k, n)) for n in range(4)] for k in range(8)]
                    y = sbt(st1, "y", [128, 8, OWN], F32)
                    y_b = [[Buf("y%d_%d" % (k, n)) for n in range(4)] for k in range(8)]
                    upad = [sbt(st1, "upad%d" % i, [128, 32 + OWN], BF16) for i in range(2)]
                    upr = Ring([(upad[i], t.dbuf("upad%d" % i)) for i in range(2)])
                    dg = [sbt(st1, "dg%d" % i, [128, 31, 128], BF16) for i in range(2)]
                    dgr = Ring([(dg[i], Buf("dg%d" % i)) for i in range(2)])
                    ybf = [sbt(st1, "ybf%d" % i, [128, 2, 512], BF16) for i in range(2)]
                    ybr = Ring([(ybf[i], Buf("ybf%d" % i)) for i in range(2)])
                    mu = sbt(st1, "mu", [128, 512], F32); mu_b = Buf("mu")
                    rs = sbt(st1, "rs", [128, 512], F32); rs_b = Buf("rs")
                    nmr = sbt(st1, "nmr", [128, 512], F32); nmr_b = Buf("nmr")
                    tmp = [sbt(st1, "tmpc%d" % i, [128, 512], F32) for i in range(2)]
                    tmpr = Ring([(tmp[i], Buf("tmpc%d" % i)) for i in range(2)])
                    cbank = Ring([0, 1, 2, 3])
                    for c in range(8):
                        up, up_b = upr.next()
                        t.dma("sp", lambda: nc.sync.dma_start(out=up[:], in_=u_d[c, :, TT - 32:3 * TT]), u_b, up_b)
                        dgt, dg_b = dgr.next()
                        for j in range(31):
                            t.op("dve", lambda: nc.vector.tensor_scalar(out=dgt[:, j, :], in0=identb[:],
                                                                        scalar1=cwb[:, c * 31 + j:c * 31 + j + 1], scalar2=None,
                                                                        op0=ALU.mult), reads=[identb_b, cwb_b], writes=[dg_b])
                        for nt in range(4):
                            bk = cbank.next()

                            def mm():
                                for j in range(31):
                                    ins = nc.tensor.matmul(P[bk][:, :], lhsT=dgt[:, j, :], rhs=up[:, 2 + nt * 512 + j:2 + nt * 512 + j + 512],
                                                           start=(j == 0), stop=(j == 30))
                                return ins
                            t.op("pe", mm, reads=[dg_b, up_b], writes=[PB[bk]])
                            t.op("act", lambda: nc.scalar.activation(out=y[:, c, nt * 512:(nt + 1) * 512], in_=P[bk][:, :],
                                                                     func=AF.Identity, bias=vecs[:, V_CB + c:V_CB + c + 1], scale=1.0),
                                 reads=[PB[bk], vecs_b], writes=[y_b[c][nt]])
                    for nt in range(4):
                        sl = slice(nt * 512, (nt + 1) * 512)
                        for c in range(8):
                            yb, yb_b = ybr.next()
                            t.op("dve", lambda: nc.vector.tensor_copy(out=yb[:, 0, :], in_=y[:, c, sl]), reads=[y_b[c][nt]], writes=[yb_b])
                            t.op("act", lambda: nc.scalar.activation(out=yb[:, 1, :], in_=y[:, c, sl], func=AF.Square),
                                 reads=[y_b[c][nt]], writes=[yb_b])
                            t.op("pe", lambda: nc.tensor.matmul(P[4][:, :], lhsT=onesb[:], rhs=yb[:, 0, :], start=(c == 0), stop=(c == 7)),
                                 reads=[yb_b, onesb_b], writes=[PB[4]])
                            t.op("pe", lambda: nc.tensor.matmul(P[5][:, :], lhsT=onesb[:], rhs=yb[:, 1, :], start=(c == 0), stop=(c == 7)),
                                 reads=[yb_b, onesb_b], writes=[PB[5]])
                        t.op("dve", lambda: nc.vector.tensor_scalar(out=mu[:], in0=P[4][:, :], scalar1=1.0 / 1024, scalar2=None, op0=ALU.mult),
                             reads=[PB[4]], writes=[mu_b])
                        t.op("dve", lambda: nc.vector.tensor_tensor(out=nmr[:], in0=mu[:], in1=mu[:], op=ALU.mult), reads=[mu_b], writes=[nmr_b])
                        t.op("dve", lambda: nc.vector.scalar_tensor_tensor(out=rs[:], in0=P[5][:, :], scalar=1.0 / 1024, in1=nmr[:],
                                                                           op0=ALU.mult, op1=ALU.subtract),
                             reads=[PB[5], nmr_b], writes=[rs_b])
                        t.op("act", lambda: nc.scalar.activation(out=rs[:], in_=rs[:], func=AF.Sqrt, bias=epsc[:, 0:1], scale=1.0),
                             reads=[rs_b, epsc_b], writes=[rs_b])
                        t.op("dve", lambda: nc.vector.reciprocal(out=rs[:], in_=rs[:]), reads=[rs_b], writes=[rs_b])
                        t.op("dve", lambda: nc.vector.scalar_tensor_tensor(out=nmr[:], in0=mu[:], scalar=-1.0, in1=rs[:],
                                                                           op0=ALU.mult, op1=ALU.mult),
                             reads=[mu_b, rs_b], writes=[nmr_b])
                        for c in range(8):
                            tm, tm_b = tmpr.next()
                            t.op("dve", lambda: nc.vector.tensor_tensor(out=tm[:], in0=y[:, c, sl], in1=rs[:], op=ALU.mult),
                                 reads=[y_b[c][nt], rs_b], writes=[tm_b])
                            t.op("dve", lambda: nc.vector.tensor_tensor(out=tm[:], in0=tm[:], in1=nmr[:], op=ALU.add),
                                 reads=[tm_b, nmr_b], writes=[tm_b])
                            t.op("act", lambda: nc.scalar.activation(out=mixT[:, c, sl], in_=tm[:], func=AF.Silu,
                                                                     bias=vecs[:, V_LB + c:V_LB + c + 1], scale=vecs[:, V_LG + c:V_LG + c + 1]),
                                 reads=[tm_b, vecs_b], writes=[mixT_b[c][nt]])
                    for ti in range(2):
                        t.op("sp", lambda: nc.sync.dma_start(out=mix_d[ti][:, 0:8, :], in_=mixT[:, :, ti * TT:(ti + 1) * TT]),
                             reads=[mixT_b[k][n] for k in range(8) for n in range(4)], writes=[mix_b], sem=mix_b.sem, nowaw=True)
                    t.barrier()
                    t.release()
                with ExitStack() as st2_:
                  if not os.environ.get('SKIP_ATTN'):
                    qTs = [sbt(st2_, "qTs%d" % i, [128, OWN], BF16) for i in range(2)]
                    qr = Ring([(qTs[i], t.dbuf("qTs%d" % i)) for i in range(2)])
                    kTs = [sbt(st2_, "kTs%d" % i, [128, LOC], BF16) for i in range(2)]
                    kr = Ring([(kTs[i], t.dbuf("kTs%d" % i)) for i in range(2)])
                    Vd2 = [[sbt(st2_, "Vd%d_%d" % (k, i), [128, 32, 260], BF16) for i in range(3)] for k in range(2)]
                    Vd2_b = [[t.dbuf("Vd%d_%d" % (k, i), sw=True) for i in range(3)] for k in range(2)]
                    tabs = [sbt(st2_, "tab%d" % i, [128, 2, 9 * 128], BF16) for i in range(2)]
                    tabr = Ring([(tabs[i], t.dbuf("tabA%d" % i, sw=True), t.dbuf("tabB%d" % i, sw=True)) for i in range(2)])
                    PT = [sbt(st2_, "PT%d" % i, [128, 256], BF16) for i in range(6)]
                    ptr = Ring([(PT[i], Buf("PT%d" % i)) for i in range(6)])
                    accs = [sbt(st2_, "acc%d" % i, [128, OWN], F32) for i in range(2)]
                    accr = Ring([(accs[i], t.dbuf("acc%d" % i)) for i in range(2)])
                    dnb = [sbt(st2_, "dnb%d" % i, [64, OWN], F32) for i in range(2)]
                    dnr = Ring([(dnb[i], t.dbuf("dnb%d" % i)) for i in range(2)])
                    obs = [sbt(st2_, "ob%d" % i, [64, OWN], BF16) for i in range(2)]
                    obr = Ring([(obs[i], t.dbuf("ob%d" % i)) for i in range(2)])
                    den_b = Buf("den_d")
                    den2_b = Buf("den2_d")
                    rts = [sbt(st2_, "rt%d" % i, [128, 16], F32) for i in range(2)]
                    rtr = Ring([(rts[i], t.dbuf("rt%d" % i)) for i in range(2)])
                    sbanks = Ring([0, 1, 6])
                    ndbanks = Ring([2, 3, 4, 5, 7])

                    def sig(d, j, i0=0):
                        return (j // d) * (128 * d) + (j % d) + d * i0

                    def vview(d):
                        if d == 1:
                            return v_d.rearrange("(b i) c -> i b c", i=128)
                        return v_d.rearrange("(b i r) c -> i b r c", i=128, r=d)

                    def load_v(hg):
                        Vd, Vd_b = Vd2[hg % 2], Vd2_b[hg % 2]
                        for di, d in enumerate(DILS):
                            vv = vview(d)
                            if d == 1:
                                for b8 in range(4):
                                    t.dma("pool", lambda: nc.gpsimd.dma_start(out=Vd[di][:, b8 * 8:(b8 + 1) * 8, :],
                                                                              in_=vv[:, b8 * 8:(b8 + 1) * 8, hg * 260:(hg + 1) * 260]),
                                          v_b, Vd_b[di], nowaw=True)
                            else:
                                nb = 32 // d
                                for bb in range(nb):
                                    t.dma("pool", lambda: nc.gpsimd.dma_start(out=Vd[di][:, bb * d:(bb + 1) * d, :],
                                                                              in_=vv[:, bb, :, hg * 260:(hg + 1) * 260]), v_b, Vd_b[di],
                                          nowaw=True)

                    def load_pair(c):
                        qT, q_b = qr.next()
                        kT, k_b = kr.next()
                        t.dma("sp", lambda: nc.sync.dma_start(out=qT[:], in_=qT_d[c]), qT_b, q_b)
                        t.dma("sp", lambda: nc.sync.dma_start(out=kT[:], in_=kT_d[c]), kT_b, k_b)
                        tb, tbA, tbB = tabr.next()
                        for hh, tbx in ((0, tbA), (1, tbB)):
                            h = 2 * c + hh
                            t.dma("pool", lambda: nc.gpsimd.dma_start(out=tb[:, hh, :], in_=tab_d[:, h * 1152:(h + 1) * 1152]), None, tbx)
                        return (qT, q_b, kT, k_b, tb, (tbA, tbB))

                    def epilogue(h, c, pb, acc, acc_b):
                        t.store("sp", lambda: nc.sync.dma_start(out=den_d[h:h + 1, :], in_=acc[64:65, :]), acc_b, den_b)
                        rt, rt_b = rtr.next()
                        t.dma("sp", lambda: nc.sync.dma_start(out=rt[:], in_=den_d[h:h + 1, :].rearrange("o (p f) -> (o p) f", p=128)),
                              den_b, rt_b)
                        t.op("dve", lambda: nc.vector.reciprocal(out=rt[:], in_=rt[:]), reads=[rt_b], writes=[rt_b])
                        t.store("sp", lambda: nc.sync.dma_start(out=den2_d[h:h + 1, :].rearrange("o (p f) -> (o p) f", p=128), in_=rt[:]),
                                rt_b, den2_b)
                        dn, dn_b = dnr.next()
                        t.dma("sp", lambda: nc.sync.dma_start(out=dn[:], in_=den2_d[h:h + 1, :].partition_broadcast(64)), den2_b, dn_b)
                        ob, ob_b = obr.next()
                        t.op("dve", lambda: nc.vector.tensor_tensor(out=ob[:], in0=acc[0:64, :], in1=dn[:], op=ALU.mult),
                             reads=[acc_b, dn_b], writes=[ob_b])
                        for ti in range(2):
                            t.store("sp", lambda: nc.sync.dma_start(out=mix_d[ti][pb:pb + 64, 8 + c, :], in_=ob[:, ti * TT:(ti + 1) * TT]),
                                    ob_b, mix_b)

                    def emit_pv(item):
                        (di, d, jq, pt, pt_b, bank, Vd, Vd_b, vc0, acc, acc_b, h, c, pb) = item
                        j = 16 + jq
                        qi = jq % 4

                        def mm():
                            nc.tensor.matmul(P[bank][0:65, qi * 128:(qi + 1) * 128], lhsT=Vd[di][:, j - d, vc0:vc0 + 65],
                                             rhs=pt[:, 0:128], start=True, stop=False)
                            return nc.tensor.matmul(P[bank][0:65, qi * 128:(qi + 1) * 128], lhsT=Vd[di][:, j, vc0:vc0 + 65],
                                                    rhs=pt[:, 128:256], start=False, stop=True)
                        t.op("pe", mm, reads=[Vd_b[di], pt_b], writes=[PB[bank]])
                        if qi == 3:
                            g4 = jq // 4
                            src = P[bank][0:65, :].rearrange("p (r i) -> p r i", r=4)
                            if d == 1:
                                dst = acc[0:65, g4 * 512:(g4 + 1) * 512].rearrange("p (r i) -> p r i", r=4)
                            elif d == 4:
                                dst = acc[0:65, g4 * 512:(g4 + 1) * 512].rearrange("p (i r) -> p r i", r=4)
                            else:
                                dst = acc[0:65, :].rearrange("p (i r) -> p r i", r=16)[:, g4 * 4:(g4 + 1) * 4, :]
                            if di == 0:
                                t.op("dve", lambda: nc.vector.tensor_copy(out=dst, in_=src), reads=[PB[bank]], writes=[acc_b])
                            else:
                                t.op("dve", lambda: nc.vector.tensor_tensor(out=dst, in0=src, in1=dst, op=ALU.add),
                                     reads=[PB[bank], acc_b], writes=[acc_b])
                            if di == 2 and jq == 15:
                                epilogue(h, c, pb, acc, acc_b)

                    pend = []
                    load_v(0)
                    nxt = load_pair(0)
                    for hg in range(4):
                        Vd, Vd_b = Vd2[hg % 2], Vd2_b[hg % 2]
                        for cp in range(2):
                            c = hg * 2 + cp
                            (qT, q_b, kT, k_b, tb, tbs) = nxt
                            if c + 1 < 8:
                                nxt = load_pair(c + 1)
                            if cp == 1 and hg + 1 < 4:
                                load_v(hg + 1)
                            for hh in range(2):
                                tbx = tbs[hh]
                                h = 2 * c + hh
                                pb = 64 * hh
                                vc0 = (cp * 2 + hh) * 65
                                acc, acc_b = accr.next()
                                cur_nd = None
                                for (di, d, jq) in [(di, d, jq) for di, d in enumerate(DILS) for jq in range(16)]:
                                    j = 16 + jq
                                    if jq % 4 == 0:
                                        cur_nd = ndbanks.next()
                                    sbk = sbanks.next()
                                    kprev = 0 if jq < d else 1
                                    q0 = sig(d, j) - HALO
                                    kp0, kc0 = sig(d, j - d), sig(d, j)
                                    if kprev == 1:
                                        brhs = tb[:, hh, (di * 3 + 1) * 128:(di * 3 + 3) * 128]
                                    else:
                                        brhs = tb[:, hh, di * 384:(di + 1) * 384].rearrange("p (k n) -> p k n", k=3)[:, 0:3:2, :]

                                    def sc():
                                        nc.tensor.matmul(P[sbk][:, 0:256], lhsT=identb[:], rhs=brhs, start=True, stop=False)
                                        nc.tensor.matmul(P[sbk][:, 0:128], lhsT=kT[pb:pb + 64, kp0:kp0 + 127 * d + 1:d],
                                                         rhs=qT[pb:pb + 64, q0:q0 + 127 * d + 1:d], start=False, stop=False)
                                        return nc.tensor.matmul(P[sbk][:, 128:256], lhsT=kT[pb:pb + 64, kc0:kc0 + 127 * d + 1:d],
                                                                rhs=qT[pb:pb + 64, q0:q0 + 127 * d + 1:d], start=False, stop=True)
                                    t.op("pe", sc, reads=[identb_b, tbx, k_b, q_b], writes=[PB[sbk]])
                                    pt, pt_b = ptr.next()
                                    t.op("act", lambda: nc.scalar.activation(out=pt[:], in_=P[sbk][:, 0:256], func=AF.Exp),
                                         reads=[PB[sbk]], writes=[pt_b])
                                    pend.append((di, d, jq, pt, pt_b, cur_nd, Vd, Vd_b, vc0, acc, acc_b, h, c, pb))
                                    if len(pend) > 3:
                                        emit_pv(pend.pop(0))
                    while pend:
                        emit_pv(pend.pop(0))
                    t.barrier()
                    t.release()

        if stop_after == "C":
            Wout_v = Wout.rearrange("(kc p) f -> p kc f", p=128)
            with ExitStack() as stC:
                B = alloc_ffn_bufs(stC)
                hT, hT_b, xT, xT_b = B["hT"], B["hT_b"], B["xT"], B["xT_b"]
                allx = [xT_b[k][n] for k in range(KC) for n in range(2)]
                allh = [hT_b[k][n] for k in range(KC) for n in range(2)]
                for ti in range(2):
                    t.op("sp", lambda: nc.sync.dma_start(out=xT[:], in_=x1_d[ti]), reads=[x1_b], writes=allx, sem=x1_b.sem)
                    t.op("sp", lambda: nc.sync.dma_start(out=hT[:], in_=mix_d[ti]), reads=[mix_b], writes=allh, sem=mix_b.sem)
                    pend = []
                    for dc in range(KC):
                        slab, ba, _bb = B["wgu"].next()
                        t.dma("pool", lambda: nc.gpsimd.dma_start(out=slab[:, 0], in_=Wout_v[:, :, dc * 128:(dc + 1) * 128]), None, ba)
                        for nt in range(2):
                            sl = slice(nt * 512, (nt + 1) * 512)
                            bk = B["dn_banks"].next()

                            def mm():
                                for kc in range(KC):
                                    ins = nc.tensor.matmul(P[bk][:, :], lhsT=slab[:, 0, kc, :], rhs=hT[:, kc, sl],
                                                           start=(kc == 0), stop=(kc == KC - 1))
                                return ins
                            t.op("pe", mm, reads=[ba] + [hT_b[kc][nt] for kc in range(KC)], writes=[PB[bk]])
                            t.op("dve", lambda: nc.vector.tensor_tensor(out=xT[:, dc, sl], in0=P[bk][:, :], in1=xT[:, dc, sl], op=ALU.add),
                                 reads=[PB[bk], xT_b[dc][nt]], writes=[xT_b[dc][nt]])
                        for f in pend:
                            f()
                        pend = [stat_sq(B, dc, nt) for nt in range(2)]
                    for nt in range(2):
                        for kc in range(KC):
                            make_h(B, V_G2, kc, nt, kc)

                    def tail(pend=pend):
                        for f in pend:
                            f()
                        stat_finish(B, 0)
                        stat_finish(B, 1)
                    B["defer"].append(tail)
                    ffn(B, Wg2, Wu2, Wd2, None)
                    for s in range(TT // 128):
                        nt = s // 4
                        ost, ost_b = B["xs"].next()
                        for k4 in range(4):
                            bk = B["misc_banks"].next()

                            def tr():
                                for j in range(4):
                                    kc = k4 * 4 + j
                                    ins = nc.tensor.transpose(out=P[bk][:, j * 128:(j + 1) * 128], in_=xT[:, kc, s * 128:(s + 1) * 128],
                                                              identity=identf[:])
                                return ins
                            t.op("pe", tr, reads=[identf_b] + [xT_b[k4 * 4 + j][nt] for j in range(4)], writes=[PB[bk]])
                            evac(k4, ost[:, k4 * 512:(k4 + 1) * 512], P[bk][:, :], [PB[bk]], [ost_b])
                        r0 = ti * TT + s * 128
                        t.store("sp", lambda: nc.sync.dma_start(out=out_d[r0:r0 + 128, :], in_=ost[:]), ost_b, out_b)
                t.barrier()
        t.barrier()
    return nc


def make_tables(seq_start):
    i = np.arange(128, dtype=np.float64)[:, None]
    n = np.arange(128, dtype=np.float64)[None, :]
    tab = np.empty((128, 16, 3, 3, 128), dtype=np.float32)
    for h in range(16):
        slope = 2.0 ** (-8.0 * (h + 1) / 16)
        for di, d in enumerate(DILS):
            dprev = n + 128 - i
            prev = np.where(dprev <= 128, -slope * d * dprev, NEG)
            dcur = n - i
            cur = np.where(dcur >= 0, -slope * d * dcur, NEG)
            tab[:, h, di, 0, :] = NEG if seq_start else prev
            tab[:, h, di, 1, :] = prev
            tab[:, h, di, 2, :] = cur
    return np.ascontiguousarray(tab.reshape(128, 16 * 9 * 128))


def make_inputs(inputs):
    f = lambda k: np.ascontiguousarray(np.asarray(inputs[k], dtype=np.float32))
    x = f("x").reshape(2 * 8192, D)
    col = lambda v, n: np.asarray(v, np.float32).reshape(n, 128).T
    vecs = np.zeros((128, NV), np.float32)
    vecs[:, V_G1:V_G1 + 16] = col(f("ffn1_norm_g")[0], 16)
    vecs[:, V_GM:V_GM + 16] = col(f("mix_norm_g")[0], 16)
    vecs[:, V_G2:V_G2 + 16] = col(f("ffn2_norm_g")[0], 16)
    vecs[:, V_CB:V_CB + 8] = col(f("conv_b_dw")[0], 8)
    vecs[:, V_LG:V_LG + 8] = col(f("conv_ln_g")[0], 8)
    vecs[:, V_LB:V_LB + 8] = col(f("conv_ln_b")[0], 8)
    vecs[:, V_QG] = np.tile(f("q_norm_g")[0], 2)
    vecs[:, V_KG] = np.tile(f("k_norm_g")[0], 2)
    cw = f("conv_w_dw")[0]
    vecs[:, V_CW:V_CW + 248] = cw.reshape(31, 8, 128).transpose(2, 1, 0).reshape(128, 248)
    shared = {
        "wg1": f("ffn1_w_gate")[0], "wu1": f("ffn1_w_up")[0], "wd1": f("ffn1_w_down")[0],
        "wg2": f("ffn2_w_gate")[0], "wu2": f("ffn2_w_up")[0], "wd2": f("ffn2_w_down")[0],
        "win": f("w_in")[0], "wout": f("w_out")[0], "vecs": vecs, "ident": np.eye(128, dtype=np.float32),
    }
    tabs = {True: make_tables(True), False: make_tables(False)}
    in_maps = []
    for c in range(NCORES):
        start = (c % 4 == 0)
        r0 = c * OWN
        xin = np.zeros((LOC, D), np.float32)
        if not start:
            xin[:HALO] = x[r0 - HALO:r0]
        xin[HALO:] = x[r0:r0 + OWN]
        m = dict(shared)
        m["xin"] = xin
        m["tab"] = tabs[start]
        in_maps.append(m)
    return in_maps


def kernel(**inputs):
    in_maps = make_inputs(inputs)
    nc = build_nc()
    res = run_bass_kernel_spmd(nc, in_maps, core_ids=list(range(NCORES)))
    out = np.concatenate([np.asarray(r["out"], dtype=np.float32) for r in res.results], axis=0)
    return out.reshape(2, 8192, D)
```

```python
import os
import numpy as np
from contextlib import ExitStack
import concourse.bass as bass
import concourse.mybir as mybir
from concourse.bass_utils import run_bass_kernel_spmd

F32 = mybir.dt.float32
BF16 = mybir.dt.bfloat16
AF = mybir.ActivationFunctionType
ALU = mybir.AluOpType

D = 2048
FF = 5632
KC = 16
FCN = 44
G = 11
NG = 4
TT = 1024
OWN = 2048
HALO = 2048
LOC = OWN + HALO
NCORES = 8
EPS = 1e-6
NEG = -30000.0
DILS = (1, 4, 16)
V_G1, V_GM, V_G2, V_CB, V_LG, V_LB, V_QG, V_KG, V_CW = 0, 16, 32, 48, 56, 64, 72, 73, 74
NV = 74 + 8 * 31


class Sem:
    def __init__(self, nc, name):
        self.h = nc.alloc_semaphore(name)
        self.v = 0


class Buf:
    __slots__ = ("name", "w", "r", "sem", "excl")

    def __init__(self, name, sem=None, excl=False):
        self.name = name
        self.w = None
        self.r = {}
        self.sem = sem
        self.excl = excl


class Trk:
    def __init__(self, nc):
        self.nc = nc
        self.eng = {"pe": nc.tensor, "act": nc.scalar, "dve": nc.vector, "pool": nc.gpsimd, "sp": nc.sync}
        self.esem = {k: Sem(nc, "s_" + k) for k in ["pe", "act", "dve", "pool"]}
        self.waited = {}
        self.dsems = []
        self.free = {False: [], True: []}
        self.local = {False: [], True: []}

    def dbuf(self, name, persistent=False, sw=False):
        free = self.free[sw]
        if persistent or not free:
            s = Sem(self.nc, "d%d_%s" % (len(self.dsems), name))
            self.dsems.append(s)
        else:
            s = free.pop()
        if not persistent:
            self.local[sw].append(s)
        return Buf(name, s)

    def release(self):
        for k in (False, True):
            self.free[k].extend(self.local[k])
            self.local[k] = []

    def store(self, q, fn, src, dst):
        return self.op(q, fn, reads=[src], writes=[dst], sem=src.sem, nowaw=True)

    def _wait(self, e, ev):
        sem, val = ev
        key = (e, id(sem))
        if self.waited.get(key, 0) >= val:
            return
        self.waited[key] = val
        self.eng[e].wait_ge(sem.h, val)

    def op(self, e, fn, reads=(), writes=(), sem=None, nowaw=False):
        if sem is None:
            own, inc = self.esem[e], 1
        else:
            own, inc = sem, 16
        for b in reads:
            if b.w is not None and not (b.w[0] is own and e == "pe"):
                self._wait(e, b.w)
            if b.excl:
                for ev in b.r.values():
                    if ev[0] is not own:
                        self._wait(e, ev)
        for b in writes:
            if b.w is not None and not nowaw and not (b.w[0] is own and e == "pe"):
                self._wait(e, b.w)
            for ev in b.r.values():
                if ev[0] is own and e == "pe":
                    continue
                self._wait(e, ev)
        ins = fn()
        ins.then_inc(own.h, inc)
        own.v += inc
        ev = (own, own.v)
        for b in reads:
            b.r[id(own)] = ev
        for b in writes:
            b.w = ev
            b.r = {}
        return ev

    def dma(self, q, fn, src, dst, nowaw=False):
        return self.op(q, fn, reads=[src] if src is not None else [], writes=[dst], sem=dst.sem, nowaw=nowaw)

    def barrier(self):
        sems = list(self.esem.values()) + self.dsems
        for e in self.eng:
            for s in sems:
                if s.v > 0:
                    self._wait(e, (s, s.v))


class Ring:
    def __init__(self, items):
        self.items = items
        self.i = 0

    def next(self):
        it = self.items[self.i % len(self.items)]
        self.i += 1
        return it


def build_nc(debug=False, stop_after="C"):
    nc = bass.Bass("TRN2", target_bir_lowering=False)
    dt_in = lambda n, s, d=F32: nc.dram_tensor(n, s, d, kind="ExternalInput").ap()
    xin = dt_in("xin", [LOC, D])
    Wg1, Wu1, Wd1 = dt_in("wg1", [D, FF]), dt_in("wu1", [D, FF]), dt_in("wd1", [FF, D])
    Wg2, Wu2, Wd2 = dt_in("wg2", [D, FF]), dt_in("wu2", [D, FF]), dt_in("wd2", [FF, D])
    Win, Wout = dt_in("win", [D, 5120]), dt_in("wout", [D, D])
    vecs_d = dt_in("vecs", [128, NV])
    ident_d = dt_in("ident", [128, 128])
    tab_d = dt_in("tab", [128, 16 * 9 * 128])
    out_d = nc.dram_tensor("out", [OWN, D], F32, kind="ExternalOutput").ap()
    skind = "ExternalOutput" if debug else "Internal"
    x1_d = nc.dram_tensor("x1_d", [2, 128, KC, TT], F32, kind=skind).ap()
    kT_d = nc.dram_tensor("kT_d", [8, 128, LOC], BF16, kind=skind).ap()
    qT_d = nc.dram_tensor("qT_d", [8, 128, OWN], BF16, kind=skind).ap()
    v_d = nc.dram_tensor("v_d", [LOC, 1040], BF16, kind=skind).ap()
    den_d = nc.dram_tensor("den_d", [16, OWN], F32).ap()
    den2_d = nc.dram_tensor("den2_d", [16, OWN], F32).ap()
    u_d = nc.dram_tensor("u_d", [8, 128, 3 * TT], BF16, kind=skind).ap()
    mix_d = nc.dram_tensor("mix_d", [2, 128, KC, TT], BF16, kind=skind).ap()

    with ExitStack() as es:
        t = Trk(nc)
        cnt = [0]

        def sbt(st, n, s, d):
            cnt[0] += 1
            return st.enter_context(nc.sbuf_tensor("sb%d_%s" % (cnt[0], n), s, d))
        P = [es.enter_context(nc.psum_tensor("ps%d" % i, [128, 512], F32)) for i in range(8)]
        PB = [Buf("ps%d" % i, excl=True) for i in range(8)]
        vecs = sbt(es, "vecs", [128, NV], F32); vecs_b = t.dbuf("vecs", True)
        identf = sbt(es, "identf", [128, 128], F32); identf_b = t.dbuf("identf", True)
        identb = sbt(es, "identb", [128, 128], BF16); identb_b = Buf("identb")
        onesb = sbt(es, "onesb", [128, 128], BF16); onesb_b = Buf("onesb")
        blkb = sbt(es, "blkb", [128, 128], BF16); blkb_b = Buf("blkb")
        epsc = sbt(es, "epsc", [128, 1], F32); epsc_b = Buf("epsc")
        c128 = sbt(es, "c128", [128, 1], F32); c128_b = Buf("c128")
        gq8 = sbt(es, "gq8", [128, 1], F32); gq8_b = Buf("gq8")
        cwb = sbt(es, "cwb", [128, 8 * 31], BF16); cwb_b = Buf("cwb")
        out_b = Buf("out")
        x1_b, mix_b = t.dbuf("x1d", True), t.dbuf("mixd", True)
        kT_b, qT_b, v_b, u_b = Buf("kTd"), Buf("qTd"), Buf("vd"), Buf("ud")

        t.dma("sp", lambda: nc.sync.dma_start(out=vecs[:], in_=vecs_d), None, vecs_b)
        t.dma("sp", lambda: nc.sync.dma_start(out=identf[:], in_=ident_d), None, identf_b)
        t.op("dve", lambda: nc.vector.tensor_copy(out=identb[:], in_=identf[:]), reads=[identf_b], writes=[identb_b])
        t.op("dve", lambda: nc.vector.memset(onesb[:], 1.0), writes=[onesb_b])
        t.op("dve", lambda: nc.vector.memset(blkb[:], 0.0), writes=[blkb_b])
        t.op("dve", lambda: nc.vector.memset(blkb[0:64, 0:64], 1.0), writes=[blkb_b])
        t.op("dve", lambda: nc.vector.memset(blkb[64:128, 64:128], 1.0), writes=[blkb_b])
        t.op("dve", lambda: nc.vector.memset(epsc[:], EPS), writes=[epsc_b])
        t.op("dve", lambda: nc.vector.memset(c128[:], 1.0 / 128), writes=[c128_b])
        t.op("dve", lambda: nc.vector.tensor_scalar(out=gq8[:], in0=vecs[:, V_QG:V_QG + 1], scalar1=0.125, scalar2=None,
                                                    op0=ALU.mult), reads=[vecs_b], writes=[gq8_b])
        t.op("dve", lambda: nc.vector.tensor_copy(out=cwb[:], in_=vecs[:, V_CW:V_CW + 248]), reads=[vecs_b], writes=[cwb_b])

        def alloc_ffn_bufs(st):
            B = {}
            B["xT"] = sbt(st, "xT", [128, KC, TT], F32)
            B["xT_b"] = [[Buf("xT%d_%d" % (k, n)) for n in range(2)] for k in range(KC)]
            B["hT"] = sbt(st, "hT", [128, KC, TT], BF16)
            B["hT_b"] = [[Buf("hT%d_%d" % (k, n)) for n in range(2)] for k in range(KC)]
            B["act"] = sbt(st, "act", [128, G, TT], BF16)
            B["act_b"] = [[Buf("act%d_%d" % (k, n)) for n in range(2)] for k in range(G)]
            wgu = [sbt(st, "wgu%d" % i, [128, 2, KC, 128], BF16) for i in range(3)]
            B["wgu"] = Ring([(wgu[i], t.dbuf("wg%d" % i, sw=True), t.dbuf("wu%d" % i, sw=True)) for i in range(3)])
            wd = [sbt(st, "wd%d" % i, [128, G, 128], BF16) for i in range(3)]
            B["wd"] = Ring([(wd[i], t.dbuf("wd%d" % i, sw=True)) for i in range(3)])
            xs = [sbt(st, "xs%d" % i, [128, D], F32) for i in range(2)]
            B["xs"] = Ring([(xs[i], t.dbuf("xs%d" % i)) for i in range(2)])
            sq = [sbt(st, "sq%d" % i, [128, 512], BF16) for i in range(2)]
            B["sq"] = Ring([(sq[i], Buf("sq%d" % i)) for i in range(2)])
            B["rstd"] = sbt(st, "rstd", [128, TT], F32); B["rstd_b"] = [Buf("rstd0"), Buf("rstd1")]
            B["ve"] = sbt(st, "ve", [128, TT], F32); B["ve_b"] = [Buf("ve0"), Buf("ve1")]
            g1 = [sbt(st, "g1_%d" % i, [128, 512], F32) for i in range(2)]
            B["g1"] = Ring([(g1[i], Buf("g1_%d" % i)) for i in range(2)])
            sg = [sbt(st, "sg%d" % i, [128, 512], F32) for i in range(2)]
            B["sg"] = Ring([(sg[i], Buf("sg%d" % i)) for i in range(2)])
            t1 = [sbt(st, "t1_%d" % i, [128, 512], F32) for i in range(2)]
            B["t1"] = Ring([(t1[i], Buf("t1_%d" % i)) for i in range(2)])
            B["gu_banks"] = Ring([(0, 1), (2, 3)])
            B["dn_banks"] = Ring([4, 5])
            B["misc_banks"] = Ring([6, 7])
            B["defer"] = []
            return B

        def flush_defer(B):
            fs, B["defer"] = B["defer"], []
            for f in fs:
                f()

        def evac(idx, out, in_, reads, writes):
            if idx % 2 == 0:
                t.op("act", lambda: nc.scalar.copy(out=out, in_=in_), reads=reads, writes=writes)
            else:
                t.op("dve", lambda: nc.vector.tensor_copy(out=out, in_=in_), reads=reads, writes=writes)

        def make_h(B, gcol, kc, nt, idx):
            xT, xT_b, hT, hT_b = B["xT"], B["xT_b"], B["hT"], B["hT_b"]
            sl = slice(nt * 512, (nt + 1) * 512)
            gap = vecs[:, gcol + kc:gcol + kc + 1]
            if idx % 2 == 0 and not os.environ.get('DBG_NOACTH'):
                t.op("act", lambda: nc.scalar.activation(out=hT[:, kc, sl], in_=xT[:, kc, sl], func=AF.Identity, scale=gap),
                     reads=[xT_b[kc][nt], vecs_b], writes=[hT_b[kc][nt]])
            else:
                t.op("dve", lambda: nc.vector.tensor_scalar(out=hT[:, kc, sl], in0=xT[:, kc, sl], scalar1=gap, scalar2=None,
                                                            op0=ALU.mult), reads=[xT_b[kc][nt], vecs_b], writes=[hT_b[kc][nt]])

        def stat_sq(B, kc, nt, in_act=False):
            xT, xT_b = B["xT"], B["xT_b"]
            sl = slice(nt * 512, (nt + 1) * 512)
            if in_act:
                sq, sq_b = B["act"][:, kc // 2, (kc % 2) * 512:(kc % 2 + 1) * 512], B["act_b"][kc // 2][kc % 2]
            else:
                sq0, sq_b = B["sq"].next()
                sq = sq0[:]
            t.op("act", lambda: nc.scalar.activation(out=sq, in_=xT[:, kc, sl], func=AF.Square), reads=[xT_b[kc][nt]], writes=[sq_b])

            def pe_part():
                t.op("pe", lambda: nc.tensor.matmul(P[6 + nt][:, :], lhsT=onesb[:], rhs=sq, start=(kc == 0), stop=(kc == KC - 1)),
                     reads=[sq_b, onesb_b], writes=[PB[6 + nt]])
            return pe_part

        def stat_finish(B, nt):
            rstd, ve = B["rstd"], B["ve"]
            sl = slice(nt * 512, (nt + 1) * 512)
            t.op("act", lambda: nc.scalar.activation(out=rstd[:, sl], in_=P[6 + nt][:, :], func=AF.Sqrt, bias=epsc[:, 0:1],
                                                     scale=1.0 / D), reads=[PB[6 + nt], epsc_b], writes=[B["rstd_b"][nt]])
            t.op("dve", lambda: nc.vector.scalar_tensor_tensor(out=ve[:, sl], in0=rstd[:, sl], scalar=EPS, in1=rstd[:, sl],
                                                               op0=ALU.mult, op1=ALU.mult),
                 reads=[B["rstd_b"][nt]], writes=[B["ve_b"][nt]])
            t.op("dve", lambda: nc.vector.reciprocal(out=rstd[:, sl], in_=rstd[:, sl]), reads=[B["rstd_b"][nt]],
                 writes=[B["rstd_b"][nt]])

        def load_transpose(B, row0, gcol):
            xT, xT_b = B["xT"], B["xT_b"]
            tr_banks = Ring([4, 5])
            pend = []
            for s in range(TT // 128):
                xs, xs_b = B["xs"].next()
                t.dma("sp", lambda: nc.sync.dma_start(out=xs[:], in_=xin[row0 + s * 128: row0 + (s + 1) * 128, :]), None, xs_b)
                nt = s // 4
                for k4 in range(4):
                    bk = tr_banks.next()

                    def tr():
                        for j in range(4):
                            kc = k4 * 4 + j
                            ins = nc.tensor.transpose(out=P[bk][:, j * 128:(j + 1) * 128], in_=xs[:, kc * 128:(kc + 1) * 128],
                                                      identity=identf[:])
                        return ins
                    t.op("pe", tr, reads=[xs_b, identf_b], writes=[PB[bk]])
                    evac(k4, xT[:, k4 * 4:(k4 + 1) * 4, s * 128:(s + 1) * 128], P[bk][:, :].rearrange("p (j t) -> p j t", j=4),
                         [PB[bk]], [xT_b[k4 * 4 + j][nt] for j in range(4)])
                    if pend:
                        for f in pend[:2]:
                            f()
                        pend = pend[2:]
                        if not pend and not os.environ.get('DBG_NOSTAT'):
                            stat_finish(B, 0)
                if s % 4 == 3:
                    fs = []
                    for kc in range(KC):
                        if not os.environ.get('DBG_NOH'):
                            make_h(B, gcol, kc, nt, kc + 1)
                        if not os.environ.get('DBG_NOSTAT'):
                            fs.append(stat_sq(B, kc, nt, in_act=True))
                    if nt == 0:
                        pend = fs
                    else:
                        B["defer"].append(lambda fs=fs: [f() for f in fs] and None)
            assert not pend
            if not os.environ.get('DBG_NOSTAT'):
                B["defer"].append(lambda: stat_finish(B, 1))

        def gated_pair(B, Wa_v, ca, Wb_v, cb, func, out_fn, out_bufs_fn, cols=((0, 512), (512, 512))):
            hT, hT_b = B["hT"], B["hT_b"]
            rstd, rstd_b = B["rstd"], B["rstd_b"]
            slab, ba, bb = B["wgu"].next()
            t.dma("pool", lambda: nc.gpsimd.dma_start(out=slab[:, 0], in_=Wa_v[:, :, ca:ca + 128]), None, ba)
            t.dma("pool", lambda: nc.gpsimd.dma_start(out=slab[:, 1], in_=Wb_v[:, :, cb:cb + 128]), None, bb)
            for ci, (c0, n) in enumerate(cols):
                sl = slice(c0, c0 + n)
                nt = c0 // 512
                pa, pb = B["gu_banks"].next()
                for (bank, half, wb) in ((pa, 0, ba), (pb, 1, bb)):
                    def mm():
                        for kc in range(KC):
                            ins = nc.tensor.matmul(P[bank][:, 0:n], lhsT=slab[:, half, kc, :], rhs=hT[:, kc, sl],
                                                   start=(kc == 0), stop=(kc == KC - 1))
                        return ins
                    t.op("pe", mm, reads=[wb] + [hT_b[kc][nt] for kc in range(KC)], writes=[PB[bank]])
                flush_defer(B)
                g1, g1_b = B["g1"].next()
                sg, sg_b = B["sg"].next()
                t1, t1_b = B["t1"].next()
                t.op("dve", lambda: nc.vector.tensor_tensor(out=g1[:, 0:n], in0=P[pb][:, 0:n], in1=rstd[:, sl], op=ALU.mult),
                     reads=[PB[pb], rstd_b[nt]], writes=[g1_b])
                t.op("act", lambda: nc.scalar.activation(out=sg[:, 0:n], in_=g1[:, 0:n], func=func), reads=[g1_b], writes=[sg_b])
                t.op("dve", lambda: nc.vector.tensor_tensor(out=t1[:, 0:n], in0=P[pa][:, 0:n], in1=rstd[:, sl], op=ALU.mult),
                     reads=[PB[pa], rstd_b[nt]], writes=[t1_b])
                t.op("dve", lambda: nc.vector.tensor_tensor(out=out_fn(ci), in0=t1[:, 0:n], in1=sg[:, 0:n], op=ALU.mult),
                     reads=[t1_b, sg_b], writes=out_bufs_fn(ci))

        def ffn(B, Wg, Wu, Wd, next_gcol):
            xT, xT_b, act, act_b = B["xT"], B["xT_b"], B["act"], B["act_b"]
            Wg_v = Wg.rearrange("(kc p) f -> p kc f", p=128)
            Wu_v = Wu.rearrange("(kc p) f -> p kc f", p=128)
            Wd_v = Wd.rearrange("(fc p) d -> p fc d", p=128)
            for gi in range(NG):
                for fl in range(G):
                    fc = gi * G + fl
                    gated_pair(B, Wu_v, fc * 128, Wg_v, fc * 128, AF.Silu,
                               lambda nt: act[:, fl, nt * 512:(nt + 1) * 512], lambda nt: [act_b[fl][nt]])
                last = (gi == NG - 1) and next_gcol is not None
                pend = []
                for dc in range(KC):
                    slab, sb_ = B["wd"].next()
                    t.dma("pool", lambda: nc.gpsimd.dma_start(out=slab[:], in_=Wd_v[:, gi * G:(gi + 1) * G, dc * 128:(dc + 1) * 128]),
                          None, sb_)
                    for nt in range(2):
                        sl = slice(nt * 512, (nt + 1) * 512)
                        bk = B["dn_banks"].next()

                        def mm():
                            for fl in range(G):
                                ins = nc.tensor.matmul(P[bk][:, :], lhsT=slab[:, fl, :], rhs=act[:, fl, sl],
                                                       start=(fl == 0), stop=(fl == G - 1))
                            return ins
                        t.op("pe", mm, reads=[sb_] + [act_b[fl][nt] for fl in range(G)], writes=[PB[bk]])
                        t.op("dve", lambda: nc.vector.scalar_tensor_tensor(out=xT[:, dc, sl], in0=P[bk][:, :], scalar=0.5,
                                                                           in1=xT[:, dc, sl], op0=ALU.mult, op1=ALU.add),
                             reads=[PB[bk], xT_b[dc][nt]], writes=[xT_b[dc][nt]])
                    if last:
                        for f in pend:
                            f()
                        pend = []
                        for nt in range(2):
                            make_h(B, next_gcol, dc, nt, nt)
                            pend.append(stat_sq(B, dc, nt))
                if last:
                    def tail(pend=pend):
                        for f in pend:
                            f()
                        stat_finish(B, 0)
                        stat_finish(B, 1)
                    B["defer"].append(tail)

        Win_v = Win.rearrange("(kc p) f -> p kc f", p=128)
        with ExitStack() as stA:
            B = alloc_ffn_bufs(stA)
            st2 = [sbt(stA, "st2_%d" % i, [128, TT], BF16) for i in range(2)]
            st2r = Ring([(st2[i], t.dbuf("st2_%d" % i)) for i in range(2)])
            vst = sbt(stA, "vst", [128, 4, 1040], BF16); vst_b = t.dbuf("vst")
            t.op("dve", lambda: nc.vector.memset(vst[:], 1.0), writes=[vst_b])
            rcol = sbt(stA, "rcol", [128, 8], F32); rcol_b = Buf("rcol")
            hT, hT_b, xT, xT_b = B["hT"], B["hT_b"], B["xT"], B["xT_b"]

            qk_banks = Ring([(0, 1), (2, 3), (4, 5)])

            def qk_jobs(jobs):
                steps = []
                for (col0, gcol_ap, gbuf, dst_ap, dst_b) in jobs:
                    st = {"stg": None}
                    for nt in range(2):
                        steps.append((col0, gcol_ap, gbuf, dst_ap, dst_b, nt, st))
                pend = []

                def stage1(step):
                    (col0, gcol_ap, gbuf, dst_ap, dst_b, nt, st) = step
                    if nt == 0:
                        slab, ba, _bb = B["wgu"].next()
                        t.dma("pool", lambda: nc.gpsimd.dma_start(out=slab[:, 0], in_=Win_v[:, :, col0:col0 + 128]), None, ba)
                        st["slab"], st["ba"] = slab, ba
                        st["stg"], st["stg_b"] = st2r.next()
                    slab, ba = st["slab"], st["ba"]
                    sl = slice(nt * 512, (nt + 1) * 512)
                    pa, pb = qk_banks.next()

                    def mm():
                        for kc in range(KC):
                            ins = nc.tensor.matmul(P[pa][:, :], lhsT=slab[:, 0, kc, :], rhs=hT[:, kc, sl],
                                                   start=(kc == 0), stop=(kc == KC - 1))
                        return ins
                    t.op("pe", mm, reads=[ba] + [hT_b[kc][nt] for kc in range(KC)], writes=[PB[pa]])
                    flush_defer(B)
                    sq, sq_b = B["sq"].next()
                    t.op("act", lambda: nc.scalar.activation(out=sq[:, 0:512], in_=P[pa][:, :], func=AF.Square),
                         reads=[PB[pa]], writes=[sq_b])
                    return (step, pa, pb, sq, sq_b)

                def stage2(item):
                    (step, pa, pb, sq, sq_b) = item
                    (col0, gcol_ap, gbuf, dst_ap, dst_b, nt, st) = step
                    stg, stg_b = st["stg"], st["stg_b"]
                    sl = slice(nt * 512, (nt + 1) * 512)
                    t.op("pe", lambda: nc.tensor.matmul(P[pb][:, :], lhsT=blkb[:], rhs=sq[:, 0:512], start=True, stop=True),
                         reads=[sq_b, blkb_b], writes=[PB[pb]])
                    sg, sg_b = B["sg"].next()
                    t.op("dve", lambda: nc.vector.scalar_tensor_tensor(out=sg[:], in0=P[pb][:, :], scalar=1.0 / 64, in1=B["ve"][:, sl],
                                                                       op0=ALU.mult, op1=ALU.add),
                         reads=[PB[pb], B["ve_b"][nt]], writes=[sg_b])
                    t.op("act", lambda: nc.scalar.activation(out=sg[:], in_=sg[:], func=AF.Sqrt), reads=[sg_b], writes=[sg_b])
                    t.op("dve", lambda: nc.vector.reciprocal(out=sg[:], in_=sg[:]), reads=[sg_b], writes=[sg_b])
                    t.op("dve", lambda: nc.vector.scalar_tensor_tensor(out=stg[:, sl], in0=P[pa][:, :], scalar=gcol_ap, in1=sg[:],
                                                                       op0=ALU.mult, op1=ALU.mult),
                         reads=[PB[pa], sg_b, gbuf], writes=[stg_b])
                    if nt == 1:
                        t.store("sp", lambda: nc.sync.dma_start(out=dst_ap, in_=stg[:]), stg_b, dst_b)

                for step in steps:
                    pend.append(stage1(step))
                    if len(pend) > 1:
                        stage2(pend.pop(0))
                while pend:
                    stage2(pend.pop(0))

            for ti in range(4):
                own = ti >= 2
                load_transpose(B, ti * TT, V_G1)
                if os.environ.get("DBG_STOP") == "load":
                    flush_defer(B)
                    break
                ffn(B, Wg1, Wu1, Wd1, V_GM)
                if os.environ.get("DBG_STOP") == "ffn":
                    flush_defer(B)
                    break
                jobs = [(3072 + c * 128, vecs[:, V_KG:V_KG + 1], vecs_b, kT_d[c, :, ti * TT:(ti + 1) * TT], kT_b) for c in range(8)]
                if own:
                    jobs += [(2048 + c * 128, gq8[:, 0:1], gq8_b, qT_d[c, :, (ti - 2) * TT:(ti - 1) * TT], qT_b) for c in range(8)]
                qk_jobs(jobs)
                if os.environ.get("DBG_STOP") == "qk":
                    break
                for q4 in range(2):
                    def rc():
                        for j in range(4):
                            s_ = q4 * 4 + j
                            ins = nc.tensor.transpose(out=P[6 + q4][:, j * 128:(j + 1) * 128], in_=B["rstd"][:, s_ * 128:(s_ + 1) * 128],
                                                      identity=identf[:])
                        return ins
                    t.op("pe", rc, reads=[B["rstd_b"][q4], identf_b], writes=[PB[6 + q4]])
                    t.op("dve", lambda: nc.vector.tensor_copy(out=rcol[:, q4 * 4:(q4 + 1) * 4], in_=P[6 + q4][:, 0:512:128]),
                         reads=[PB[6 + q4]], writes=[rcol_b])
                for half in range(2):
                    for qc in range(4):
                        slab4, sb_, _bb = B["wgu"].next()
                        slab = slab4[:].rearrange("p a k c -> p (a k c)").rearrange("p (k c) -> p k c", c=256)
                        t.dma("pool", lambda: nc.gpsimd.dma_start(out=slab, in_=Win_v[:, :, 4096 + qc * 256:4096 + (qc + 1) * 256]),
                              None, sb_)
                        for s2 in range(2):
                            bk = B["dn_banks"].next()

                            def mm():
                                for h2 in range(2):
                                    s = half * 4 + s2 * 2 + h2
                                    for kc in range(KC):
                                        ins = nc.tensor.matmul(P[bk][:, h2 * 256:(h2 + 1) * 256], lhsT=hT[:, kc, s * 128:(s + 1) * 128],
                                                               rhs=slab[:, kc, :], start=(kc == 0), stop=(kc == KC - 1))
                                return ins
                            t.op("pe", mm, reads=[sb_] + [hT_b[kc][half] for kc in range(KC)], writes=[PB[bk]])
                            for h2 in range(2):
                                s = half * 4 + s2 * 2 + h2
                                o_ = vst[:, s2 * 2 + h2, qc * 260:(qc + 1) * 260].rearrange("p (h c) -> p h c", c=65)[:, :, 0:64]
                                i_ = P[bk][:, h2 * 256:(h2 + 1) * 256].rearrange("p (h c) -> p h c", c=64)
                                if s2 == 0:
                                    t.op("act", lambda: nc.scalar.activation(out=o_, in_=i_, func=AF.Identity, scale=rcol[:, s:s + 1]),
                                         reads=[PB[bk], rcol_b], writes=[vst_b])
                                else:
                                    t.op("dve", lambda: nc.vector.tensor_scalar(out=o_, in0=i_, scalar1=rcol[:, s:s + 1], scalar2=None,
                                                                                op0=ALU.mult), reads=[PB[bk], rcol_b], writes=[vst_b])
                    r0 = ti * TT + half * 512
                    t.store("sp", lambda: nc.sync.dma_start(out=v_d[r0:r0 + 512, :].rearrange("(s p) c -> p s c", p=128),
                                                            in_=vst[:]), vst_b, v_b)
                if os.environ.get("DBG_STOP") == "v":
                    break
                if ti >= 1:
                    for c in range(8):
                        stg, stg_b = st2r.next()
                        if ti == 1:
                            gated_pair(B, Win_v, c * 128, Win_v, 1024 + c * 128, AF.Sigmoid,
                                       lambda ci: stg[:, TT - 128:TT], lambda ci: [stg_b], cols=((TT - 128, 128),))
                            t.store("sp", lambda: nc.sync.dma_start(out=u_d[c, :, TT - 128:TT], in_=stg[:, TT - 128:TT]), stg_b, u_b)
                        else:
                            gated_pair(B, Win_v, c * 128, Win_v, 1024 + c * 128, AF.Sigmoid,
                                       lambda nt: stg[:, nt * 512:(nt + 1) * 512], lambda nt: [stg_b])
                            t.store("sp", lambda: nc.sync.dma_start(out=u_d[c, :, (ti - 1) * TT:ti * TT], in_=stg[:]), stg_b, u_b)
                if own:
                    t.op("act", lambda: nc.scalar.dma_start(out=x1_d[ti - 2], in_=xT[:]),
                         reads=[xT_b[k][n] for k in range(KC) for n in range(2)], writes=[x1_b], sem=x1_b.sem, nowaw=True)
            t.barrier()
            t.release()

        if stop_after != "A":
            with ExitStack() as stB:
                with ExitStack() as st1:
                  if not os.environ.get('SKIP_CONV'):
                    mixT = sbt(st1, "mixT", [128, 8, OWN], BF16)
                    mixT_b = [[Buf("mix%d_%d" % (k, n)) for n in range(4)] for k in range(8)]
                    y = sbt(st1, "y", [128, 8, OWN], F32)
                    y_b = [[Buf("y%d_%d" % (k, n)) for n in range(4)] for k in range(8)]
                    upad = [sbt(st1, "upad%d" % i, [128, 32 + OWN], BF16) for i in range(2)]
                    upr = Ring([(upad[i], t.dbuf("upad%d" % i)) for i in range(2)])
                    dg = [sbt(st1, "dg%d" % i, [128, 31, 128], BF16) for i in range(2)]
                    dgr = Ring([(dg[i], Buf("dg%d" % i)) for i in range(2)])
                    ybf = [sbt(st1, "ybf%d" % i, [128, 2, 512], BF16) for i in range(2)]
                    ybr = Ring([(ybf[i], Buf("ybf%d" % i)) for i in range(2)])
                    mu = sbt(st1, "mu", [128, 512], F32); mu_b = Buf("mu")
                    rs = sbt(st1, "rs", [128, 512], F32); rs_b = Buf("rs")
                    nmr = sbt(st1, "nmr", [128, 512], F32); nmr_b = Buf("nmr")
                    tmp = [sbt(st1, "tmpc%d" % i, [128, 512], F32) for i in range(2)]
                    tmpr = Ring([(tmp[i], Buf("tmpc%d" % i)) for i in range(2)])
                    cbank = Ring([0, 1, 2, 3])
                    for c in range(8):
                        up, up_b = upr.next()
                        t.dma("sp", lambda: nc.sync.dma_start(out=up[:], in_=u_d[c, :, TT - 32:3 * TT]), u_b, up_b)
                        dgt, dg_b = dgr.next()
                        for j in range(31):
                            t.op("dve", lambda: nc.vector.tensor_scalar(out=dgt[:, j, :], in0=identb[:],
                                                                        scalar1=cwb[:, c * 31 + j:c * 31 + j + 1], scalar2=None,
                                                                        op0=ALU.mult), reads=[identb_b, cwb_b], writes=[dg_b])
                        for nt in range(4):
                            bk = cbank.next()

                            def mm():
                                for j in range(31):
                                    ins = nc.tensor.matmul(P[bk][:, :], lhsT=dgt[:, j, :], rhs=up[:, 2 + nt * 512 + j:2 + nt * 512 + j + 512],
                                                           start=(j == 0), stop=(j == 30))
                                return ins
                            t.op("pe", mm, reads=[dg_b, up_b], writes=[PB[bk]])
                            t.op("act", lambda: nc.scalar.activation(out=y[:, c, nt * 512:(nt + 1) * 512], in_=P[bk][:, :],
                                                                     func=AF.Identity, bias=vecs[:, V_CB + c:V_CB + c + 1], scale=1.0),
                                 reads=[PB[bk], vecs_b], writes=[y_b[c][nt]])
                    for nt in range(4):
                        sl = slice(nt * 512, (nt + 1) * 512)
                        for c in range(8):
                            yb, yb_b = ybr.next()
                            t.op("dve", lambda: nc.vector.tensor_copy(out=yb[:, 0, :], in_=y[:, c, sl]), reads=[y_b[c][nt]], writes=[yb_b])
                            t.op("act", lambda: nc.scalar.activation(out=yb[:, 1, :], in_=y[:, c, sl], func=AF.Square),
                                 reads=[y_b[c][nt]], writes=[yb_b])
                            t.op("pe", lambda: nc.tensor.matmul(P[4][:, :], lhsT=onesb[:], rhs=yb[:, 0, :], start=(c == 0), stop=(c == 7)),
                                 reads=[yb_b, onesb_b], writes=[PB[4]])
                            t.op("pe", lambda: nc.tensor.matmul(P[5][:, :], lhsT=onesb[:], rhs=yb[:, 1, :], start=(c == 0), stop=(c == 7)),
                                 reads=[yb_b, onesb_b], writes=[PB[5]])
                        t.op("dve", lambda: nc.vector.tensor_scalar(out=mu[:], in0=P[4][:, :], scalar1=1.0 / 1024, scalar2=None, op0=ALU.mult),
                             reads=[PB[4]], writes=[mu_b])
                        t.op("dve", lambda: nc.vector.tensor_tensor(out=nmr[:], in0=mu[:], in1=mu[:], op=ALU.mult), reads=[mu_b], writes=[nmr_b])
                        t.op("dve", lambda: nc.vector.scalar_tensor_tensor(out=rs[:], in0=P[5][:, :], scalar=1.0 / 1024, in1=nmr[:],
                                                                           op0=ALU.mult, op1=ALU.subtract),
                             reads=[PB[5], nmr_b], writes=[rs_b])
                        t.op("act", lambda: nc.scalar.activation(out=rs[:], in_=rs[:], func=AF.Sqrt, bias=epsc[:, 0:1], scale=1.0),
                             reads=[rs_b, epsc_b], writes=[rs_b])
                        t.op("dve", lambda: nc.vector.reciprocal(out=rs[:], in_=rs[:]), reads=[rs_b], writes=[rs_b])
                        t.op("dve", lambda: nc.vector.scalar_tensor_tensor(out=nmr[:], in0=mu[:], scalar=-1.0, in1=rs[:],
                                                                           op0=ALU.mult, op1=ALU.mult),
                             reads=[mu_b, rs_b], writes=[nmr_b])
                        for c in range(8):
                            tm, tm_b = tmpr.next()
                            t.op("dve", lambda: nc.vector.tensor_tensor(out=tm[:], in0=y[:, c, sl], in1=rs[:], op=ALU.mult),
                                 reads=[y_b[c][nt], rs_b], writes=[tm_b])
                            t.op("dve", lambda: nc.vector.tensor_tensor(out=tm[:], in0=tm[:], in1=nmr[:], op=ALU.add),
                                 reads=[tm_b, nmr_b], writes=[tm_b])
                            t.op("act", lambda: nc.scalar.activation(out=mixT[:, c, sl], in_=tm[:], func=AF.Silu,
                                                                     bias=vecs[:, V_LB + c:V_LB + c + 1], scale=vecs[:, V_LG + c:V_LG + c + 1]),
                                 reads=[tm_b, vecs_b], writes=[mixT_b[c][nt]])
                    for ti in range(2):
                        t.op("sp", lambda: nc.sync.dma_start(out=mix_d[ti][:, 0:8, :], in_=mixT[:, :, ti * TT:(ti + 1) * TT]),
                             reads=[mixT_b[k][n] for k in range(8) for n in range(4)], writes=[mix_b], sem=mix_b.sem, nowaw=True)
                    t.barrier()
                    t.release()
                with ExitStack() as st2_:
                  if not os.environ.get('SKIP_ATTN'):
                    qTs = [sbt(st2_, "qTs%d" % i, [128, OWN], BF16) for i in range(2)]
                    qr = Ring([(qTs[i], t.dbuf("qTs%d" % i)) for i in range(2)])
                    kTs = [sbt(st2_, "kTs%d" % i, [128, LOC], BF16) for i in range(2)]
                    kr = Ring([(kTs[i], t.dbuf("kTs%d" % i)) for i in range(2)])
                    Vd2 = [[sbt(st2_, "Vd%d_%d" % (k, i), [128, 32, 260], BF16) for i in range(3)] for k in range(2)]
                    Vd2_b = [[t.dbuf("Vd%d_%d" % (k, i), sw=True) for i in range(3)] for k in range(2)]
                    tabs = [sbt(st2_, "tab%d" % i, [128, 2, 9 * 128], BF16) for i in range(2)]
                    tabr = Ring([(tabs[i], t.dbuf("tabA%d" % i, sw=True), t.dbuf("tabB%d" % i, sw=True)) for i in range(2)])
                    PT = [sbt(st2_, "PT%d" % i, [128, 256], BF16) for i in range(6)]
                    ptr = Ring([(PT[i], Buf("PT%d" % i)) for i in range(6)])
                    accs = [sbt(st2_, "acc%d" % i, [128, OWN], F32) for i in range(2)]
                    accr = Ring([(accs[i], t.dbuf("acc%d" % i)) for i in range(2)])
                    dnb = [sbt(st2_, "dnb%d" % i, [64, OWN], F32) for i in range(2)]
                    dnr = Ring([(dnb[i], t.dbuf("dnb%d" % i)) for i in range(2)])
                    obs = [sbt(st2_, "ob%d" % i, [64, OWN], BF16) for i in range(2)]
                    obr = Ring([(obs[i], t.dbuf("ob%d" % i)) for i in range(2)])
                    den_b = Buf("den_d")
                    den2_b = Buf("den2_d")
                    rts = [sbt(st2_, "rt%d" % i, [128, 16], F32) for i in range(2)]
                    rtr = Ring([(rts[i], t.dbuf("rt%d" % i)) for i in range(2)])
                    sbanks = Ring([0, 1, 6])
                    ndbanks = Ring([2, 3, 4, 5, 7])

                    def sig(d, j, i0=0):
                        return (j // d) * (128 * d) + (j % d) + d * i0

                    def vview(d):
                        if d == 1:
                            return v_d.rearrange("(b i) c -> i b c", i=128)
                        return v_d.rearrange("(b i r) c -> i b r c", i=128, r=d)

                    def load_v(hg):
                        Vd, Vd_b = Vd2[hg % 2], Vd2_b[hg % 2]
                        for di, d in enumerate(DILS):
                            vv = vview(d)
                            if d == 1:
                                for b8 in range(4):
                                    t.dma("pool", lambda: nc.gpsimd.dma_start(out=Vd[di][:, b8 * 8:(b8 + 1) * 8, :],
                                                                              in_=vv[:, b8 * 8:(b8 + 1) * 8, hg * 260:(hg + 1) * 260]),
                                          v_b, Vd_b[di], nowaw=True)
                            else:
                                nb = 32 // d
                                for bb in range(nb):
                                    t.dma("pool", lambda: nc.gpsimd.dma_start(out=Vd[di][:, bb * d:(bb + 1) * d, :],
                                                                              in_=vv[:, bb, :, hg * 260:(hg + 1) * 260]), v_b, Vd_b[di],
                                          nowaw=True)

                    def load_pair(c):
                        qT, q_b = qr.next()
                        kT, k_b = kr.next()
                        t.dma("sp", lambda: nc.sync.dma_start(out=qT[:], in_=qT_d[c]), qT_b, q_b)
                        t.dma("sp", lambda: nc.sync.dma_start(out=kT[:], in_=kT_d[c]), kT_b, k_b)
                        tb, tbA, tbB = tabr.next()
                        for hh, tbx in ((0, tbA), (1, tbB)):
                            h = 2 * c + hh
                            t.dma("pool", lambda: nc.gpsimd.dma_start(out=tb[:, hh, :], in_=tab_d[:, h * 1152:(h + 1) * 1152]), None, tbx)
                        return (qT, q_b, kT, k_b, tb, (tbA, tbB))

                    def epilogue(h, c, pb, acc, acc_b):
                        t.store("sp", lambda: nc.sync.dma_start(out=den_d[h:h + 1, :], in_=acc[64:65, :]), acc_b, den_b)
                        rt, rt_b = rtr.next()
                        t.dma("sp", lambda: nc.sync.dma_start(out=rt[:], in_=den_d[h:h + 1, :].rearrange("o (p f) -> (o p) f", p=128)),
                              den_b, rt_b)
                        t.op("dve", lambda: nc.vector.reciprocal(out=rt[:], in_=rt[:]), reads=[rt_b], writes=[rt_b])
                        t.store("sp", lambda: nc.sync.dma_start(out=den2_d[h:h + 1, :].rearrange("o (p f) -> (o p) f", p=128), in_=rt[:]),
                                rt_b, den2_b)
                        dn, dn_b = dnr.next()
                        t.dma("sp", lambda: nc.sync.dma_start(out=dn[:], in_=den2_d[h:h + 1, :].partition_broadcast(64)), den2_b, dn_b)
                        ob, ob_b = obr.next()
                        t.op("dve", lambda: nc.vector.tensor_tensor(out=ob[:], in0=acc[0:64, :], in1=dn[:], op=ALU.mult),
                             reads=[acc_b, dn_b], writes=[ob_b])
                        for ti in range(2):
                            t.store("sp", lambda: nc.sync.dma_start(out=mix_d[ti][pb:pb + 64, 8 + c, :], in_=ob[:, ti * TT:(ti + 1) * TT]),
                                    ob_b, mix_b)

                    def emit_pv(item):
                        (di, d, jq, pt, pt_b, bank, Vd, Vd_b, vc0, acc, acc_b, h, c, pb) = item
                        j = 16 + jq
                        qi = jq % 4

                        def mm():
                            nc.tensor.matmul(P[bank][0:65, qi * 128:(qi + 1) * 128], lhsT=Vd[di][:, j - d, vc0:vc0 + 65],
                                             rhs=pt[:, 0:128], start=True, stop=False)
                            return nc.tensor.matmul(P[bank][0:65, qi * 128:(qi + 1) * 128], lhsT=Vd[di][:, j, vc0:vc0 + 65],
                                                    rhs=pt[:, 128:256], start=False, stop=True)
                        t.op("pe", mm, reads=[Vd_b[di], pt_b], writes=[PB[bank]])
                        if qi == 3:
                            g4 = jq // 4
                            src = P[bank][0:65, :].rearrange("p (r i) -> p r i", r=4)
                            if d == 1:
                                dst = acc[0:65, g4 * 512:(g4 + 1) * 512].rearrange("p (r i) -> p r i", r=4)
                            elif d == 4:
                                dst = acc[0:65, g4 * 512:(g4 + 1) * 512].rearrange("p (i r) -> p r i", r=4)
                            else:
                                dst = acc[0:65, :].rearrange("p (i r) -> p r i", r=16)[:, g4 * 4:(g4 + 1) * 4, :]
                            if di == 0:
                                t.op("dve", lambda: nc.vector.tensor_copy(out=dst, in_=src), reads=[PB[bank]], writes=[acc_b])
                            else:
                                t.op("dve", lambda: nc.vector.tensor_tensor(out=dst, in0=src, in1=dst, op=ALU.add),
                                     reads=[PB[bank], acc_b], writes=[acc_b])
                            if di == 2 and jq == 15:
                                epilogue(h, c, pb, acc, acc_b)

                    pend = []
                    load_v(0)
                    nxt = load_pair(0)
                    for hg in range(4):
                        Vd, Vd_b = Vd2[hg % 2], Vd2_b[hg % 2]
                        for cp in range(2):
                            c = hg * 2 + cp
                            (qT, q_b, kT, k_b, tb, tbs) = nxt
                            if c + 1 < 8:
                                nxt = load_pair(c + 1)
                            if cp == 1 and hg + 1 < 4:
                                load_v(hg + 1)
                            for hh in range(2):
                                tbx = tbs[hh]
                                h = 2 * c + hh
                                pb = 64 * hh
                                vc0 = (cp * 2 + hh) * 65
                                acc, acc_b = accr.next()
                                cur_nd = None
                                for (di, d, jq) in [(di, d, jq) for di, d in enumerate(DILS) for jq in range(16)]:
                                    j = 16 + jq
                                    if jq % 4 == 0:
                                        cur_nd = ndbanks.next()
                                    sbk = sbanks.next()
                                    kprev = 0 if jq < d else 1
                                    q0 = sig(d, j) - HALO
                                    kp0, kc0 = sig(d, j - d), sig(d, j)
                                    if kprev == 1:
                                        brhs = tb[:, hh, (di * 3 + 1) * 128:(di * 3 + 3) * 128]
                                    else:
                                        brhs = tb[:, hh, di * 384:(di + 1) * 384].rearrange("p (k n) -> p k n", k=3)[:, 0:3:2, :]

                                    def sc():
                                        nc.tensor.matmul(P[sbk][:, 0:256], lhsT=identb[:], rhs=brhs, start=True, stop=False)
                                        nc.tensor.matmul(P[sbk][:, 0:128], lhsT=kT[pb:pb + 64, kp0:kp0 + 127 * d + 1:d],
                                                         rhs=qT[pb:pb + 64, q0:q0 + 127 * d + 1:d], start=False, stop=False)
                                        return nc.tensor.matmul(P[sbk][:, 128:256], lhsT=kT[pb:pb + 64, kc0:kc0 + 127 * d + 1:d],
                                                                rhs=qT[pb:pb + 64, q0:q0 + 127 * d + 1:d], start=False, stop=True)
                                    t.op("pe", sc, reads=[identb_b, tbx, k_b, q_b], writes=[PB[sbk]])
                                    pt, pt_b = ptr.next()
                                    t.op("act", lambda: nc.scalar.activation(out=pt[:], in_=P[sbk][:, 0:256], func=AF.Exp),
                                         reads=[PB[sbk]], writes=[pt_b])
                                    pend.append((di, d, jq, pt, pt_b, cur_nd, Vd, Vd_b, vc0, acc, acc_b, h, c, pb))
                                    if len(pend) > 3:
                                        emit_pv(pend.pop(0))
                    while pend:
                        emit_pv(pend.pop(0))
                    t.barrier()
                    t.release()

        if stop_after == "C":
            Wout_v = Wout.rearrange("(kc p) f -> p kc f", p=128)
            with ExitStack() as stC:
                B = alloc_ffn_bufs(stC)
                hT, hT_b, xT, xT_b = B["hT"], B["hT_b"], B["xT"], B["xT_b"]
                allx = [xT_b[k][n] for k in range(KC) for n in range(2)]
                allh = [hT_b[k][n] for k in range(KC) for n in range(2)]
                for ti in range(2):
                    t.op("sp", lambda: nc.sync.dma_start(out=xT[:], in_=x1_d[ti]), reads=[x1_b], writes=allx, sem=x1_b.sem)
                    t.op("sp", lambda: nc.sync.dma_start(out=hT[:], in_=mix_d[ti]), reads=[mix_b], writes=allh, sem=mix_b.sem)
                    pend = []
                    for dc in range(KC):
                        slab, ba, _bb = B["wgu"].next()
                        t.dma("pool", lambda: nc.gpsimd.dma_start(out=slab[:, 0], in_=Wout_v[:, :, dc * 128:(dc + 1) * 128]), None, ba)
                        for nt in range(2):
                            sl = slice(nt * 512, (nt + 1) * 512)
                            bk = B["dn_banks"].next()

                            def mm():
                                for kc in range(KC):
                                    ins = nc.tensor.matmul(P[bk][:, :], lhsT=slab[:, 0, kc, :], rhs=hT[:, kc, sl],
                                                           start=(kc == 0), stop=(kc == KC - 1))
                                return ins
                            t.op("pe", mm, reads=[ba] + [hT_b[kc][nt] for kc in range(KC)], writes=[PB[bk]])
                            t.op("dve", lambda: nc.vector.tensor_tensor(out=xT[:, dc, sl], in0=P[bk][:, :], in1=xT[:, dc, sl], op=ALU.add),
                                 reads=[PB[bk], xT_b[dc][nt]], writes=[xT_b[dc][nt]])
                        for f in pend:
                            f()
                        pend = [stat_sq(B, dc, nt) for nt in range(2)]
                    for nt in range(2):
                        for kc in range(KC):
                            make_h(B, V_G2, kc, nt, kc)

                    def tail(pend=pend):
                        for f in pend:
                            f()
                        stat_finish(B, 0)
                        stat_finish(B, 1)
                    B["defer"].append(tail)
                    ffn(B, Wg2, Wu2, Wd2, None)
                    for s in range(TT // 128):
                        nt = s // 4
                        ost, ost_b = B["xs"].next()
                        for k4 in range(4):
                            bk = B["misc_banks"].next()

                            def tr():
                                for j in range(4):
                                    kc = k4 * 4 + j
                                    ins = nc.tensor.transpose(out=P[bk][:, j * 128:(j + 1) * 128], in_=xT[:, kc, s * 128:(s + 1) * 128],
                                                              identity=identf[:])
                                return ins
                            t.op("pe", tr, reads=[identf_b] + [xT_b[k4 * 4 + j][nt] for j in range(4)], writes=[PB[bk]])
                            evac(k4, ost[:, k4 * 512:(k4 + 1) * 512], P[bk][:, :], [PB[bk]], [ost_b])
                        r0 = ti * TT + s * 128
                        t.store("sp", lambda: nc.sync.dma_start(out=out_d[r0:r0 + 128, :], in_=ost[:]), ost_b, out_b)
                t.barrier()
        t.barrier()
    return nc


def make_tables(seq_start):
    i = np.arange(128, dtype=np.float64)[:, None]
    n = np.arange(128, dtype=np.float64)[None, :]
    tab = np.empty((128, 16, 3, 3, 128), dtype=np.float32)
    for h in range(16):
        slope = 2.0 ** (-8.0 * (h + 1) / 16)
        for di, d in enumerate(DILS):
            dprev = n + 128 - i
            prev = np.where(dprev <= 128, -slope * d * dprev, NEG)
            dcur = n - i
            cur = np.where(dcur >= 0, -slope * d * dcur, NEG)
            tab[:, h, di, 0, :] = NEG if seq_start else prev
            tab[:, h, di, 1, :] = prev
            tab[:, h, di, 2, :] = cur
    return np.ascontiguousarray(tab.reshape(128, 16 * 9 * 128))


def make_inputs(inputs):
    f = lambda k: np.ascontiguousarray(np.asarray(inputs[k], dtype=np.float32))
    x = f("x").reshape(2 * 8192, D)
    col = lambda v, n: np.asarray(v, np.float32).reshape(n, 128).T
    vecs = np.zeros((128, NV), np.float32)
    vecs[:, V_G1:V_G1 + 16] = col(f("ffn1_norm_g")[0], 16)
    vecs[:, V_GM:V_GM + 16] = col(f("mix_norm_g")[0], 16)
    vecs[:, V_G2:V_G2 + 16] = col(f("ffn2_norm_g")[0], 16)
    vecs[:, V_CB:V_CB + 8] = col(f("conv_b_dw")[0], 8)
    vecs[:, V_LG:V_LG + 8] = col(f("conv_ln_g")[0], 8)
    vecs[:, V_LB:V_LB + 8] = col(f("conv_ln_b")[0], 8)
    vecs[:, V_QG] = np.tile(f("q_norm_g")[0], 2)
    vecs[:, V_KG] = np.tile(f("k_norm_g")[0], 2)
    cw = f("conv_w_dw")[0]
    vecs[:, V_CW:V_CW + 248] = cw.reshape(31, 8, 128).transpose(2, 1, 0).reshape(128, 248)
    shared = {
        "wg1": f("ffn1_w_gate")[0], "wu1": f("ffn1_w_up")[0], "wd1": f("ffn1_w_down")[0],
        "wg2": f("ffn2_w_gate")[0], "wu2": f("ffn2_w_up")[0], "wd2": f("ffn2_w_down")[0],
        "win": f("w_in")[0], "wout": f("w_out")[0], "vecs": vecs, "ident": np.eye(128, dtype=np.float32),
    }
    tabs = {True: make_tables(True), False: make_tables(False)}
    in_maps = []
    for c in range(NCORES):
        start = (c % 4 == 0)
        r0 = c * OWN
        xin = np.zeros((LOC, D), np.float32)
        if not start:
            xin[:HALO] = x[r0 - HALO:r0]
        xin[HALO:] = x[r0:r0 + OWN]
        m = dict(shared)
        m["xin"] = xin
        m["tab"] = tabs[start]
        in_maps.append(m)
    return in_maps


def kernel(**inputs):
    in_maps = make_inputs(inputs)
    nc = build_nc()
    res = run_bass_kernel_spmd(nc, in_maps, core_ids=list(range(NCORES)))
    out = np.concatenate([np.asarray(r["out"], dtype=np.float32) for r in res.results], axis=0)
    return out.reshape(2, 8192, D)
```
